# Optimizing a Trainium2 kernel written in Bass

```python
import jax, jax.numpy as jnp
from jax import lax
import numpy as np

D_MODEL = 1024
BATCH = 2
SEQ = 16384
DEPTH = 4

CONV_WIDTH = 512
CONV_K = 3
SWA_HEADS = 8
SWA_KV_HEADS = 2
SWA_HEAD_DIM = 64
WINDOW = 128
SWA_BLOCK = 128
GLA_HEADS = 4
GLA_DK = 64
GLA_DV = 128
GLA_GATE_RANK = 16
GLA_GATE_TAU = 16.0
GLA_CHUNK = 64
MLA_HEADS = 4
MLA_Q_RANK = 256
MLA_KV_RANK = 128
MLA_NOPE = 64
MLA_ROPE = 32
MLA_V = 128
MLA_BLOCK = 128
ROPE_THETA = 10000.0
D_FF = ((-(-8 * D_MODEL // 3) + 255) // 256) * 256
ALPHA = (2.0 * DEPTH) ** 0.25
BETA = (8.0 * DEPTH) ** -0.25
LN_EPS = 1e-5
RMS_EPS = 1e-6

N_EVEN = (DEPTH + 1) // 2
N_ODD = DEPTH // 2

EVEN_SPLITS = [CONV_WIDTH, CONV_WIDTH, CONV_WIDTH,
               SWA_HEADS * SWA_HEAD_DIM, SWA_KV_HEADS * SWA_HEAD_DIM, SWA_KV_HEADS * SWA_HEAD_DIM]
EVEN_IN = sum(EVEN_SPLITS)
EVEN_MIX = CONV_WIDTH + SWA_HEADS * SWA_HEAD_DIM
ODD_SPLITS = [GLA_HEADS * GLA_DK, GLA_HEADS * GLA_DK, GLA_HEADS * GLA_DV, GLA_GATE_RANK,
              GLA_HEADS * GLA_DV, MLA_Q_RANK, MLA_KV_RANK, MLA_ROPE]
ODD_IN = sum(ODD_SPLITS)
ODD_MIX = GLA_HEADS * GLA_DV + MLA_HEADS * MLA_V

kernel_name = "hybrid_conv_swa_gla_mla_deepnorm"


def split_cols(u, sizes):
    return jnp.split(u, np.cumsum(sizes)[:-1].tolist(), axis=-1)


def layer_norm(x, g, b):
    xf = x.astype(jnp.float32)
    mu = xf.mean(-1, keepdims=True)
    var = jnp.mean(jnp.square(xf - mu), -1, keepdims=True)
    return ((xf - mu) * lax.rsqrt(var + LN_EPS) * g.astype(jnp.float32) + b.astype(jnp.float32)).astype(x.dtype)


def rms_norm(x, g):
    xf = x.astype(jnp.float32)
    ms = jnp.mean(jnp.square(xf), -1, keepdims=True)
    return (xf * lax.rsqrt(ms + RMS_EPS) * g.astype(jnp.float32)).astype(x.dtype)


def apply_rope(t, cos, sin):
    t1, t2 = jnp.split(t, 2, axis=-1)
    return jnp.concatenate([t1 * cos - t2 * sin, t2 * cos + t1 * sin], axis=-1)


def short_conv_mixer(b_gate, c_gate, h, conv_w):
    S = h.shape[1]
    z = c_gate * h
    zp = jnp.pad(z, ((0, 0), (CONV_K - 1, 0), (0, 0)))
    y = zp[:, 0:S] * conv_w[0]
    for j in range(1, CONV_K):
        y = y + zp[:, j:j + S] * conv_w[j]
    return b_gate * y


def swa_sink_attention(q, k, v, sinks):
    B_, S, _ = q.shape
    nb = S // SWA_BLOCK
    G = SWA_HEADS // SWA_KV_HEADS
    q = q.reshape(B_, nb, SWA_BLOCK, SWA_KV_HEADS, G, SWA_HEAD_DIM)
    k = k.reshape(B_, nb, SWA_BLOCK, SWA_KV_HEADS, SWA_HEAD_DIM)
    v = v.reshape(B_, nb, SWA_BLOCK, SWA_KV_HEADS, SWA_HEAD_DIM)
    pad = ((0, 0), (1, 0), (0, 0), (0, 0), (0, 0))
    kk = jnp.concatenate([jnp.pad(k, pad)[:, :-1], k], axis=2)
    vv = jnp.concatenate([jnp.pad(v, pad)[:, :-1], v], axis=2)
    s = jnp.einsum('bnqhgd,bnkhd->bnhgqk', q, kk).astype(jnp.float32) * (SWA_HEAD_DIM ** -0.5)
    qi = jnp.arange(SWA_BLOCK)[:, None]
    kj = jnp.arange(2 * SWA_BLOCK)[None, :] - SWA_BLOCK
    diff = qi - kj
    band = (diff >= 0) & (diff < WINDOW)
    valid = (jnp.arange(nb)[:, None, None] > 0) | (kj[None] >= 0)
    mask = band[None] & valid
    s = jnp.where(mask[None, :, None, None], s, -jnp.inf)
    sink = sinks.astype(jnp.float32).reshape(SWA_KV_HEADS, G)[None, None, :, :, None, None]
    m = jnp.maximum(s.max(-1, keepdims=True), sink)
    p = jnp.exp(s - m)
    p = p / (p.sum(-1, keepdims=True) + jnp.exp(sink - m))
    o = jnp.einsum('bnhgqk,bnkhd->bnqhgd', p.astype(v.dtype), vv)
    return o.reshape(B_, S, SWA_HEADS * SWA_HEAD_DIM)


def gla_mixer(q, k, v, g_low, r, w_gate, b_gate, g_norm):
    B_, S, _ = q.shape
    H, C = GLA_HEADS, GLA_CHUNK
    nc = S // C
    log_a = jax.nn.log_sigmoid((g_low @ w_gate + b_gate).astype(jnp.float32)) / GLA_GATE_TAU

    def to_chunks(t, d):
        return t.astype(jnp.float32).reshape(B_, nc, C, H, d).transpose(1, 0, 3, 2, 4)

    qc = to_chunks(q, GLA_DK) * (GLA_DK ** -0.5)
    kc = to_chunks(k, GLA_DK)
    vc = to_chunks(v, GLA_DV)
    gc = to_chunks(log_a, GLA_DK)
    causal = jnp.tril(jnp.ones((C, C), bool))[:, :, None]

    def step(state, inp):
        qb, kb, vb, gb = inp
        b = jnp.cumsum(gb, axis=2)
        o_inter = jnp.einsum('bhtd,bhdv->bhtv', qb * jnp.exp(b), state)
        diff = b[:, :, :, None, :] - b[:, :, None, :, :]
        decay = jnp.exp(jnp.where(causal, diff, -jnp.inf))
        attn = jnp.einsum('bhtd,bhsd,bhtsd->bhts', qb, kb, decay)
        o = o_inter + jnp.einsum('bhts,bhsv->bhtv', attn, vb)
        b_last = b[:, :, -1:, :]
        k_dec = kb * jnp.exp(b_last - b)
        state = jnp.exp(b_last[:, :, 0, :])[..., None] * state + jnp.einsum('bhsd,bhsv->bhdv', k_dec, vb)
        return state, o

    s0 = jnp.zeros((B_, H, GLA_DK, GLA_DV), jnp.float32)
    _, o = lax.scan(step, s0, (qc, kc, vc, gc))
    o = o.transpose(1, 0, 3, 2, 4).reshape(B_, S, H, GLA_DV)
    o = rms_norm(o, g_norm).reshape(B_, S, H * GLA_DV)
    return (o * jax.nn.silu(r.astype(jnp.float32))).astype(q.dtype)


def mla_mixer(c_q, c_kv, k_r, g_qn, w_uq, g_kvn, w_ukv, cos, sin):
    B_, S, _ = c_q.shape
    H = MLA_HEADS
    q = (rms_norm(c_q, g_qn) @ w_uq).reshape(B_, S, H, MLA_NOPE + MLA_ROPE)
    q_nope = q[..., :MLA_NOPE]
    q_rope = apply_rope(q[..., MLA_NOPE:], cos[:, None, :], sin[:, None, :])
    kv = (rms_norm(c_kv, g_kvn) @ w_ukv).reshape(B_, S, H, MLA_NOPE + MLA_V)
    k_nope, v = kv[..., :MLA_NOPE], kv[..., MLA_NOPE:]
    k_rope = apply_rope(k_r, cos, sin)
    scale = (MLA_NOPE + MLA_ROPE) ** -0.5
    nb = S // MLA_BLOCK
    qn_b = q_nope.reshape(B_, nb, MLA_BLOCK, H, MLA_NOPE).transpose(1, 0, 2, 3, 4)
    qr_b = q_rope.reshape(B_, nb, MLA_BLOCK, H, MLA_ROPE).transpose(1, 0, 2, 3, 4)
    kpos = jnp.arange(S)

    def attend(args):
        qn, qr, i = args
        s = (jnp.einsum('bqhd,bkhd->bhqk', qn, k_nope) +
             jnp.einsum('bqhr,bkr->bhqk', qr, k_rope)).astype(jnp.float32) * scale
        qpos = i * MLA_BLOCK + jnp.arange(MLA_BLOCK)
        s = jnp.where(kpos[None, :] <= qpos[:, None], s, -jnp.inf)
        p = jax.nn.softmax(s, axis=-1)
        return jnp.einsum('bhqk,bkhv->bqhv', p.astype(v.dtype), v)

    o = lax.map(attend, (qn_b, qr_b, jnp.arange(nb)))
    return o.transpose(1, 0, 2, 3, 4).reshape(B_, S, H * MLA_V)


def swiglu(x, w_gate, w_up, w_down):
    return (jax.nn.silu(x @ w_gate) * (x @ w_up)) @ w_down


def setup_inputs(seed: int = 0) -> dict:
    key = jax.random.key(seed)
    ks = iter(jax.random.split(key, 32))
    nrm = lambda shape, s: jax.random.normal(next(ks), shape, jnp.float32) * s
    D = D_MODEL
    return {
        "x": nrm((BATCH, SEQ, D), 1.0),
        "ev_w_in": nrm((N_EVEN, D, EVEN_IN), D ** -0.5),
        "ev_conv_w": nrm((N_EVEN, CONV_K, CONV_WIDTH), CONV_K ** -0.5),
        "ev_sinks": nrm((N_EVEN, SWA_HEADS), 0.5),
        "ev_w_out": nrm((N_EVEN, EVEN_MIX, D), BETA * EVEN_MIX ** -0.5),
        "od_w_in": nrm((N_ODD, D, ODD_IN), D ** -0.5),
        "od_gla_w_gate": nrm((N_ODD, GLA_GATE_RANK, GLA_HEADS * GLA_DK), GLA_GATE_RANK ** -0.5),
        "od_gla_b_gate": nrm((N_ODD, GLA_HEADS * GLA_DK), 0.1),
        "od_gla_norm_g": 1.0 + nrm((N_ODD, GLA_DV), 0.02),
        "od_mla_q_norm_g": 1.0 + nrm((N_ODD, MLA_Q_RANK), 0.02),
        "od_mla_w_uq": nrm((N_ODD, MLA_Q_RANK, MLA_HEADS * (MLA_NOPE + MLA_ROPE)), MLA_Q_RANK ** -0.5),
        "od_mla_kv_norm_g": 1.0 + nrm((N_ODD, MLA_KV_RANK), 0.02),
        "od_mla_w_ukv": nrm((N_ODD, MLA_KV_RANK, MLA_HEADS * (MLA_NOPE + MLA_V)), MLA_KV_RANK ** -0.5),
        "od_w_out": nrm((N_ODD, ODD_MIX, D), BETA * ODD_MIX ** -0.5),
        "ffn_w_gate": nrm((DEPTH, D, D_FF), D ** -0.5),
        "ffn_w_up": nrm((DEPTH, D, D_FF), D ** -0.5),
        "ffn_w_down": nrm((DEPTH, D_FF, D), BETA * D_FF ** -0.5),
        "ln_mix_g": 1.0 + nrm((DEPTH, D), 0.02),
        "ln_mix_b": nrm((DEPTH, D), 0.02),
        "ln_ffn_g": 1.0 + nrm((DEPTH, D), 0.02),
        "ln_ffn_b": nrm((DEPTH, D), 0.02),
    }


def reference(x, ev_w_in, ev_conv_w, ev_sinks, ev_w_out, od_w_in, od_gla_w_gate, od_gla_b_gate,
              od_gla_norm_g, od_mla_q_norm_g, od_mla_w_uq, od_mla_kv_norm_g, od_mla_w_ukv, od_w_out,
              ffn_w_gate, ffn_w_up, ffn_w_down, ln_mix_g, ln_mix_b, ln_ffn_g, ln_ffn_b):
    S = x.shape[1]
    pos = jnp.arange(S, dtype=jnp.float32)
    inv_freq = ROPE_THETA ** (-jnp.arange(0, MLA_ROPE, 2, dtype=jnp.float32) / MLA_ROPE)
    ang = pos[:, None] * inv_freq[None, :]
    cos, sin = jnp.cos(ang).astype(x.dtype), jnp.sin(ang).astype(x.dtype)

    for layer in range(DEPTH):
        i = layer // 2
        if layer % 2 == 0:
            u = x @ ev_w_in[i]
            b_g, c_g, h, q, k, v = split_cols(u, EVEN_SPLITS)
            ya = short_conv_mixer(b_g, c_g, h, ev_conv_w[i])
            yb = swa_sink_attention(q, k, v, ev_sinks[i])
            y = jnp.concatenate([ya, yb], axis=-1) @ ev_w_out[i]
        else:
            u = x @ od_w_in[i]
            gq, gk, gv, g_low, gr, c_q, c_kv, k_r = split_cols(u, ODD_SPLITS)
            yc = gla_mixer(gq, gk, gv, g_low, gr, od_gla_w_gate[i], od_gla_b_gate[i], od_gla_norm_g[i])
            yd = mla_mixer(c_q, c_kv, k_r, od_mla_q_norm_g[i], od_mla_w_uq[i],
                           od_mla_kv_norm_g[i], od_mla_w_ukv[i], cos, sin)
            y = jnp.concatenate([yc, yd], axis=-1) @ od_w_out[i]
        x = layer_norm(ALPHA * x + y, ln_mix_g[layer], ln_mix_b[layer])
        x = layer_norm(ALPHA * x + swiglu(x, ffn_w_gate[layer], ffn_w_up[layer], ffn_w_down[layer]),
                       ln_ffn_g[layer], ln_ffn_b[layer])
    return x
```

```python
import contextlib
import numpy as np
import ml_dtypes
import concourse.bass as bass
import concourse.mybir as mybir
from concourse.bass_utils import run_bass_kernel_spmd

F32 = mybir.dt.float32
BF16 = mybir.dt.bfloat16
AF = mybir.ActivationFunctionType
ALU = mybir.AluOpType

D = 1024
NCORES = 8
SEQ = 16384
SEG = 4096
NT = SEG // 128
DFF = 2816
NFC = DFF // 128
DEPTH = 4
ALPHA = (2.0 * DEPTH) ** 0.25
LN_EPS = 1e-5
RMS_EPS = 1e-6
EVEN_IN = 2304
ODD_IN = 1968


class Res:
    __slots__ = ("name", "w", "r", "ap")

    def __init__(self, name, ap=None):
        self.name = name
        self.w = None
        self.r = {}
        self.ap = ap


class BankRes(Res):
    __slots__ = ("idx",)


class SubRes:
    __slots__ = ("name", "ap", "parent")

    def __init__(self, name, ap, parent):
        self.name = name
        self.ap = ap
        self.parent = parent


class Prog:
    ENG = ("pe", "act", "dve", "pool", "sp")

    def __init__(self, nc, es, n_dma_sems=40):
        self.nc = nc
        self.semh = {}
        for e in ("pe", "act", "dve", "pool"):
            self.semh["s_" + e] = es.enter_context(nc.semaphore("s_" + e))
        self.cnt = {e: 0 for e in ("pe", "act", "dve", "pool")}
        self.ops = {e: [] for e in self.ENG}
        self.seen = {e: {} for e in self.ENG}
        self.nd = n_dma_sems
        for i in range(n_dma_sems):
            self.semh[f"d{i}"] = es.enter_context(nc.semaphore(f"d{i}"))
        self.dval = [0] * n_dma_sems
        self.dnext = 0
        self.uid = 0
        self.nops = 0

    def _wait(self, eng, tok):
        if tok is None:
            return
        name, val = tok
        if eng == "pe" and name == "s_pe":
            return
        if self.seen[eng].get(name, 0) >= val:
            return
        self.seen[eng][name] = val
        self.ops[eng].append(("w", name, val))

    def _deps(self, eng, reads, writes):
        for r in reads:
            if r.w is not None:
                self._wait(eng, r.w)
        for w in writes:
            if w.w is not None:
                self._wait(eng, w.w)
            for t in w.r.values():
                self._wait(eng, t)

    @staticmethod
    def _norm(reads, writes):
        reads = [getattr(r, "parent", None) or r for r in reads]
        writes = [getattr(w, "parent", None) or w for w in writes]
        for r in reads:
            if isinstance(r, BankRes) and r not in writes:
                writes.append(r)
        return reads, writes

    def op(self, eng, fn, reads=(), writes=()):
        reads, writes = self._norm(reads, writes)
        self._deps(eng, reads, writes)
        self.cnt[eng] += 1
        tok = ("s_" + eng, self.cnt[eng])
        self.ops[eng].append(("o", fn, "s_" + eng, 1))
        for r in reads:
            r.r[eng] = tok
        for w in writes:
            w.w = tok
            w.r = {}
        self.nops += 1
        return tok

    def dma(self, queue, out, in_, reads=(), writes=()):
        i = self.dnext
        self.dnext = (i + 1) % self.nd
        name = f"d{i}"
        if self.dval[i] > 0:
            self._wait(queue, (name, self.dval[i]))
        self._deps(queue, reads, writes)
        self.dval[i] += 16
        tok = (name, self.dval[i])
        self.ops[queue].append(("o", lambda e, o=out, a=in_: e.dma_start(out=o, in_=a), name, 16))
        for r in reads:
            self.uid += 1
            r.r[("dma", self.uid)] = tok
        for w in writes:
            w.w = tok
            w.r = {}
        self.nops += 1
        return tok

    def barrier(self):
        toks = [("s_" + e, self.cnt[e]) for e in ("pe", "act", "dve", "pool") if self.cnt[e] > 0]
        toks += [(f"d{i}", self.dval[i]) for i in range(self.nd) if self.dval[i] > 0]
        for e in self.ENG:
            for t in toks:
                if e != "sp" and t[0] == "s_" + e:
                    continue
                self._wait(e, t)

    def finish(self):
        for i in range(self.nd):
            if self.dval[i] > 0:
                self._wait("sp", (f"d{i}", self.dval[i]))

    def emit(self):
        nc = self.nc
        semh = self.semh

        def mk(engname):
            def body(e):
                for o in self.ops[engname]:
                    if o[0] == "w":
                        e.wait_ge(semh[o[1]], o[2])
                    else:
                        o[1](e).then_inc(semh[o[2]], o[3])
            return body

        with nc.Block() as block:
            block.tensor(mk("pe"))
            block.scalar(mk("act"))
            block.vector(mk("dve"))
            block.gpsimd(mk("pool"))
            block.sync(mk("sp"))


class Arena:
    def __init__(self, ap, nwords):
        self.base = ap
        self.n = nwords
        self.off = 0

    def _shape(self, a, shape):
        if len(shape) == 1:
            return a
        if len(shape) == 2:
            return a.rearrange("p (a b) -> p a b", a=shape[0])
        if len(shape) == 3:
            return a.rearrange("p (a b c) -> p a b c", a=shape[0], b=shape[1])
        raise ValueError(shape)

    def f32(self, name, *shape):
        n = int(np.prod(shape))
        assert self.off + n <= self.n, ("arena overflow", name, self.off, n, self.n)
        a = self.base[:, self.off:self.off + n]
        self.off += n
        return Res(name, self._shape(a, shape))

    def bf16(self, name, *shape):
        n = int(np.prod(shape))
        nw = (n + 1) // 2
        assert self.off + nw <= self.n, ("arena overflow", name, self.off, nw, self.n)
        a = self.base[:, self.off:self.off + nw].bitcast(BF16)[:, 0:n]
        self.off += nw
        return Res(name, self._shape(a, shape))


class Ctx:
    pass


def mm(P, out_ap, lhsT, rhs, start, stop, reads, writes, skip=False):
    if skip:
        P.op("pe", lambda e, o=out_ap, l=lhsT, r=rhs, s=start, t=stop: e.matmul(o, l, r, start=s, stop=t, skip_group_check=True),
             reads=reads, writes=writes)
    else:
        P.op("pe", lambda e, o=out_ap, l=lhsT, r=rhs, s=start, t=stop: e.matmul(o, l, r, start=s, stop=t),
             reads=reads, writes=writes)


def emit_transposes(P, C, xt, ntile, xT, evac_flip=0):
    N = ntile * 128
    per_bank = 512 // N if N < 512 else 1
    kc = 0
    i = 0
    while kc < 8:
        bank = C.psT[C.psT_i % len(C.psT)]
        C.psT_i += 1
        g = min(per_bank, 8 - kc)
        for j in range(g):
            for t in range(ntile):
                P.op("pe", lambda e, o=bank.ap[:, j * N + t * 128: j * N + (t + 1) * 128],
                     a=xt.ap[:, t, (kc + j) * 128:(kc + j + 1) * 128], idn=C.ident.ap: e.transpose(o, a, idn),
                     reads=(xt, C.ident), writes=(bank,))
        src = bank.ap[:, 0:g * N].rearrange("p (a b) -> p a b", a=g)
        dst = xT.ap[:, kc:kc + g, 0:N]
        if (i + evac_flip) % 2 == 0:
            P.op("dve", lambda e, o=dst, a=src: e.tensor_copy(o, a), reads=(bank,), writes=(xT,))
        else:
            P.op("act", lambda e, o=dst, a=src: e.activation(out=o, in_=a, func=AF.Copy), reads=(bank,), writes=(xT,))
        kc += g
        i += 1


def emit_layernorm(P, C, xres, xres_ap, Y, gam, bet, xo):
    r, st, mv, xn = C.rbuf, C.stat, C.mv, C.xn
    yap = C.ps_all[:, Y[0].idx * 512: Y[0].idx * 512 + 1024]
    P.op("dve", lambda e, o=r.ap, a=xres_ap, y=yap: e.scalar_tensor_tensor(out=o, in0=a, scalar=ALPHA, in1=y, op0=ALU.mult, op1=ALU.add),
         reads=(xres, Y[0], Y[1]), writes=(r,))
    P.op("dve", lambda e, o=st.ap[:, 0:6], a=r.ap[:, 0:512]: e.bn_stats(o, a), reads=(r,), writes=(st,))
    P.op("dve", lambda e, o=st.ap[:, 6:12], a=r.ap[:, 512:1024]: e.bn_stats(o, a), reads=(r,), writes=(st,))
    P.op("dve", lambda e, o=mv.ap[:, 0:2], a=st.ap[:, 0:12]: e.bn_aggr(o, a), reads=(st,), writes=(mv,))
    P.op("dve", lambda e, o=mv.ap[:, 2:3], a=mv.ap[:, 1:2]: e.tensor_scalar(o, a, LN_EPS, None, op0=ALU.add),
         reads=(mv,), writes=(mv,))
    P.op("act", lambda e, o=mv.ap[:, 2:3], a=mv.ap[:, 2:3]: e.activation(out=o, in_=a, func=AF.Ln), reads=(mv,), writes=(mv,))
    P.op("act", lambda e, o=mv.ap[:, 2:3], a=mv.ap[:, 2:3]: e.activation(out=o, in_=a, func=AF.Exp, scale=-0.5), reads=(mv,), writes=(mv,))
    P.op("dve", lambda e, o=mv.ap[:, 3:4], a=mv.ap[:, 0:1], b=mv.ap[:, 2:3]: e.scalar_tensor_tensor(out=o, in0=a, scalar=-1.0, in1=b, op0=ALU.mult, op1=ALU.mult),
         reads=(mv,), writes=(mv,))
    P.op("act", lambda e, o=xn.ap, a=r.ap, b=mv.ap[:, 3:4], s=mv.ap[:, 2:3]: e.activation(out=o, in_=a, func=AF.Identity, bias=b, scale=s),
         reads=(r, mv), writes=(xn,))
    P.op("pool", lambda e, o=xn.ap, a=xn.ap, g=gam.ap: e.tensor_tensor(o, a, g, op=ALU.mult), reads=(xn, gam), writes=(xn,))
    P.op("pool", lambda e, o=xo.ap, a=xn.ap, b=bet.ap: e.tensor_tensor(o, a, b, op=ALU.add), reads=(xn, bet), writes=(xo,))


def _mkbanks(C, ps):
    C.bank = []
    for i in range(8):
        b = BankRes(f"bank{i}", ps[:, i * 512:(i + 1) * 512])
        b.idx = i
        C.bank.append(b)


def setup_common(P, C, nc, es, arena_words):
    arena_t = es.enter_context(nc.sbuf_tensor("arena", [128, arena_words], F32))
    C.arena = Arena(arena_t[:, :], arena_words)
    ps = es.enter_context(nc.psum_tensor("ps", [128, 4096], F32))
    C.ps_all = ps[:, :]
    _mkbanks(C, ps)


FFN_GROUPS = [(0, 768), (768, 1536), (1536, 2304), (2304, 2816)]


def emit_ffn_sweep(P, C, src_dram, src_res, dst_dram, dst_res, wg_d, wu_d, wd_d, ln_d, ident_d, ntile=2, ntiles=NT, dst_off=0, skip_tiles=0):
    A = C.arena
    A.off = 0
    N = ntile * 128
    Wg = [A.bf16(f"Wg{g}", 8, b - a) for g, (a, b) in enumerate(FFN_GROUPS)]
    Wu = [A.bf16(f"Wu{g}", 8, b - a) for g, (a, b) in enumerate(FFN_GROUPS)]
    Wd = [A.bf16(f"Wd{g}", (b - a) // 128, 1024) for g, (a, b) in enumerate(FFN_GROUPS)]
    xT = A.bf16("xT", 8, N)
    hT = A.bf16("hT", NFC, N)
    C.ident = A.f32("ident", 128)
    gam = A.f32("gam", 1024)
    bet = A.f32("bet", 1024)
    xt = [A.f32(f"xt{i}", ntile, 1024) for i in range(2)]
    C.rbuf = A.f32("r", 1024)
    C.xn = A.f32("xn", 1024)
    xo = [A.f32(f"xo{i}", 1024) for i in range(2)]
    sg = [A.f32(f"sg{i}", N) for i in range(2)]
    C.stat = A.f32("stat", 12)
    C.mv = A.f32("mv", 4)
    C.psT = [C.bank[0], C.bank[7]]
    C.psT_i = 0
    psG = [C.bank[1], C.bank[2]]
    psU = [C.bank[3], C.bank[4]]
    psY = [C.bank[5], C.bank[6]]

    P.dma("sp", C.ident.ap, ident_d, writes=(C.ident,))
    P.dma("sp", gam.ap, ln_d[:, 0:1024], writes=(gam,))
    P.dma("sp", bet.ap, ln_d[:, 1024:2048], writes=(bet,))
    wg3 = wg_d.rearrange("p (k n) -> p k n", k=8)
    wu3 = wu_d.rearrange("p (k n) -> p k n", k=8)
    wd3 = wd_d.rearrange("p (f n) -> p f n", f=NFC)
    blocks = [(t0, min(ntile, ntiles - t0)) for t0 in range(0, ntiles, ntile)]

    def load_x(bi):
        t0, nb = blocks[bi]
        P.dma("sp", xt[bi % 2].ap[:, 0:nb, :], src_dram[t0 * 128:(t0 + nb) * 128, :].rearrange("(t p) d -> p t d", p=128),
              reads=tuple(src_res[t0:t0 + nb]), writes=(xt[bi % 2],))

    load_x(0)
    for g, (a, b) in enumerate(FFN_GROUPS):
        P.dma("pool", Wg[g].ap, wg3[:, :, a:b], writes=(Wg[g],))
        P.dma("pool", Wu[g].ap, wu3[:, :, a:b], writes=(Wu[g],))
    for g, (a, b) in enumerate(FFN_GROUPS):
        P.dma("pool", Wd[g].ap, wd3[:, a // 128:b // 128, :], writes=(Wd[g],))

    for bi, (t0, nb) in enumerate(blocks):
        x = xt[bi % 2]
        Nb = nb * 128
        emit_transposes(P, C, x, nb, xT, evac_flip=bi)
        if bi + 1 < len(blocks):
            load_x(bi + 1)
        for fc in range(NFC):
            g = fc // 6
            c0 = (fc - g * 6) * 128
            G = psG[fc % 2]
            U = psU[fc % 2]
            for kc in range(8):
                mm(P, G.ap[:, 0:Nb], Wg[g].ap[:, kc, c0:c0 + 128], xT.ap[:, kc, 0:Nb], kc == 0, kc == 7, (Wg[g], xT), (G,))
            for kc in range(8):
                mm(P, U.ap[:, 0:Nb], Wu[g].ap[:, kc, c0:c0 + 128], xT.ap[:, kc, 0:Nb], kc == 0, kc == 7, (Wu[g], xT), (U,))
            sgt = sg[fc % 2]
            P.op("act", lambda e, o=sgt.ap[:, 0:Nb], a=G.ap[:, 0:Nb]: e.activation(out=o, in_=a, func=AF.Silu), reads=(G,), writes=(sgt,))
            P.op("dve", lambda e, o=hT.ap[:, fc, 0:Nb], a=sgt.ap[:, 0:Nb], b=U.ap[:, 0:Nb]: e.tensor_tensor(o, a, b, op=ALU.mult),
                 reads=(sgt, U), writes=(hT,))
        for t in range(nb):
            for half in range(2):
                Y = psY[half]
                for fc in range(NFC):
                    g = fc // 6
                    mm(P, Y.ap, hT.ap[:, fc, t * 128:(t + 1) * 128], Wd[g].ap[:, fc - g * 6, half * 512:(half + 1) * 512],
                       fc == 0, fc == NFC - 1, (hT, Wd[g]), (Y,))
            gt = t0 + t
            o = xo[gt % 2]
            emit_layernorm(P, C, x, x.ap[:, t, :], psY, gam, bet, o)
            if gt >= skip_tiles:
                P.dma("sp", dst_dram[(gt + dst_off) * 128:(gt + dst_off + 1) * 128, :], o.ap, reads=(o,), writes=(dst_res[gt + dst_off],))


def emit_tail_sweep(P, C, mix_d, x_d, dst_dram, dst_res, wout_d, ln_d, ident_d, ntiles):
    A = C.arena
    A.off = 0
    Wout = A.bf16("Wout", 8, 1024)
    mT = A.bf16("mT", 8, 512)
    C.ident = A.f32("ident", 128)
    gam = A.f32("gam", 1024)
    bet = A.f32("bet", 1024)
    mt = [A.f32(f"mt{i}", 4, 1024) for i in range(2)]
    xr = [A.f32(f"xr{i}", 4, 1024) for i in range(2)]
    C.rbuf = A.f32("r", 1024)
    C.xn = A.f32("xn", 1024)
    xo = [A.f32(f"xo{i}", 1024) for i in range(2)]
    C.stat = A.f32("stat", 12)
    C.mv = A.f32("mv", 4)
    C.psT = [C.bank[0], C.bank[1]]
    C.psT_i = 0
    psY = [C.bank[4], C.bank[5]]
    P.dma("sp", C.ident.ap, ident_d, writes=(C.ident,))
    P.dma("sp", gam.ap, ln_d[:, 0:1024], writes=(gam,))
    P.dma("sp", bet.ap, ln_d[:, 1024:2048], writes=(bet,))
    P.dma("pool", Wout.ap, wout_d.rearrange("p (k n) -> p k n", k=8), writes=(Wout,))
    blocks = [(t0, min(4, ntiles - t0)) for t0 in range(0, ntiles, 4)]

    def load(bi):
        t0, nb = blocks[bi]
        P.dma("sp", mt[bi % 2].ap[:, 0:nb, :], mix_d[t0 * 128:(t0 + nb) * 128, :].rearrange("(t p) d -> p t d", p=128), writes=(mt[bi % 2],))
        P.dma("sp", xr[bi % 2].ap[:, 0:nb, :], x_d[t0 * 128:(t0 + nb) * 128, :].rearrange("(t p) d -> p t d", p=128), writes=(xr[bi % 2],))

    load(0)
    for bi, (t0, nb) in enumerate(blocks):
        m = mt[bi % 2]
        x = xr[bi % 2]
        emit_transposes(P, C, m, nb, mT, evac_flip=bi)
        if bi + 1 < len(blocks):
            load(bi + 1)
        for t in range(nb):
            for half in range(2):
                Y = psY[half]
                for c in range(8):
                    mm(P, Y.ap, mT.ap[:, c, t * 128:(t + 1) * 128], Wout.ap[:, c, half * 512:(half + 1) * 512], c == 0, c == 7, (mT, Wout), (Y,))
            gt = t0 + t
            o = xo[gt % 2]
            emit_layernorm(P, C, x, x.ap[:, t, :], psY, gam, bet, o)
            P.dma("sp", dst_dram[gt * 128:(gt + 1) * 128, :], o.ap, reads=(o,), writes=(dst_res[gt],))


def emit_even_mixer_sweep(P, C, xh_d, x_d, x_res, dst_dram, dst_res, win_d, wout_d, ln_d, small_d, consts_d):
    A = C.arena
    A.off = 0
    Win = A.bf16("Win", 8, EVEN_IN)
    Wout = A.bf16("Wout", 8, 1024)
    xT = A.bf16("xT", 8, 512)
    yaT = A.bf16("yaT", 4, 512)
    ybT = A.bf16("ybT", 4, 512)
    qT = A.bf16("qT", 4, 512)
    kT = [A.bf16(f"kT{i}", 5 * 128) for i in range(2)]
    vb = [A.bf16(f"vb{i}", 5, 2, 128) for i in range(2)]
    eb = [[A.bf16(f"e{k}{j}", 512) for j in range(2)] for k in range(2)]
    maskC = A.bf16("maskC", 512)
    maskP = A.bf16("maskP", 512)
    maskPF = A.bf16("maskPF", 512)
    onesA = A.bf16("onesA", 128)
    onesB = A.bf16("onesB", 128)
    C.ident = A.f32("ident", 128)
    gam = A.f32("gam", 1024)
    bet = A.f32("bet", 1024)
    small = A.f32("small", 17)
    es4 = A.f32("es4", 4)
    esink = A.f32("esink", 4, 128)
    xt = [A.f32(f"xt{i}", 4, 1024) for i in range(2)]
    zb = [A.f32(f"zb{c}", 514) for c in range(4)]
    csb = A.f32("csb", 512)
    ycv = A.f32("ycv", 512)
    dsb = A.f32("dsb", 512)
    C.rbuf = A.f32("r", 1024)
    C.xn = A.f32("xn", 1024)
    xo = [A.f32(f"xo{i}", 1024) for i in range(2)]
    C.stat = A.f32("stat", 12)
    C.mv = A.f32("mv", 4)
    C.psT = [C.bank[0], C.bank[1]]
    C.psT_i = 0
    psI = [C.bank[0], C.bank[1]]
    psS = [[C.bank[2], C.bank[3]], [C.bank[4], C.bank[5]]]
    psNum, psDen = C.bank[6], C.bank[7]
    psY = [C.bank[4], C.bank[5]]
    pI = [0]

    def nextI():
        b = psI[pI[0] % 2]
        pI[0] += 1
        return b

    P.dma("sp", C.ident.ap, consts_d[:, 0:128], writes=(C.ident,))
    P.dma("sp", small.ap, small_d, writes=(small,))
    P.dma("sp", xt[1].ap[:, 0, :], xh_d, writes=(xt[1],))
    P.dma("pool", Win.ap, win_d.rearrange("p (k n) -> p k n", k=8), writes=(Win,))
    P.dma("pool", maskC.ap, consts_d[:, 128:640], writes=(maskC,))
    P.dma("pool", maskP.ap, consts_d[:, 640:1152], writes=(maskP,))
    P.dma("pool", onesA.ap, consts_d[:, 1152:1280], writes=(onesA,))
    P.dma("pool", onesB.ap, consts_d[:, 1280:1408], writes=(onesB,))
    P.dma("pool", Wout.ap, wout_d.rearrange("p (k n) -> p k n", k=8), writes=(Wout,))
    P.dma("sp", gam.ap, ln_d[:, 0:1024], writes=(gam,))
    P.dma("sp", bet.ap, ln_d[:, 1024:2048], writes=(bet,))
    P.op("act", lambda e, o=es4.ap, a=small.ap[:, 12:16]: e.activation(out=o, in_=a, func=AF.Exp), reads=(small,), writes=(es4,))
    P.op("dve", lambda e, o=esink.ap, a=es4.ap.unsqueeze(2).to_broadcast([128, 4, 128]): e.tensor_copy(o, a), reads=(es4,), writes=(esink,))
    P.op("dve", lambda e, o=maskPF.ap, a=maskP.ap, s=small.ap[:, 16:17]: e.tensor_scalar(o, a, s, None, op0=ALU.mult),
         reads=(maskP, small), writes=(maskPF,))
    for i in range(2):
        P.op("pool", lambda e, o=vb[i].ap: e.memset(o, 0.0), writes=(vb[i],))
    for c in range(4):
        P.op("pool", lambda e, o=zb[c].ap: e.memset(o, 0.0), writes=(zb[c],))

    def load_x(b):
        P.dma("sp", xt[b % 2].ap, x_d[b * 512:(b + 1) * 512, :].rearrange("(t p) d -> p t d", p=128),
              reads=tuple(x_res[b * 4:(b + 1) * 4]), writes=(xt[b % 2],))

    def inproj(col0, N, bank):
        for kc in range(8):
            mm(P, bank.ap[:, 0:N], Win.ap[:, kc, col0:col0 + 128], xT.ap[:, kc, 0:N], kc == 0, kc == 7, (Win, xT), (bank,))

    def block(blk):
        halo = blk < 0
        ntile = 1 if halo else 4
        N = ntile * 128
        x = xt[1] if halo else xt[blk % 2]
        cur = 0 if halo else blk % 2
        slot0 = 0 if halo else 1
        emit_transposes(P, C, x, ntile, xT, evac_flip=blk)
        if halo:
            load_x(0)
        elif blk + 1 < NT // 4:
            load_x(blk + 1)
        for ch in range(4):
            bc = nextI()
            inproj(512 + ch * 128, N, bc)
            P.op("act", lambda e, o=csb.ap[:, 0:N], a=bc.ap[:, 0:N]: e.activation(out=o, in_=a, func=AF.Copy), reads=(bc,), writes=(csb,))
            bh = nextI()
            inproj(1024 + ch * 128, N, bh)
            z = zb[ch]
            P.op("dve", lambda e, o=z.ap[:, 2:2 + N], a=csb.ap[:, 0:N], b=bh.ap[:, 0:N]: e.tensor_tensor(o, a, b, op=ALU.mult),
                 reads=(csb, bh), writes=(z,))
            if not halo:
                bb = nextI()
                inproj(ch * 128, N, bb)
                w = small.ap
                P.op("dve", lambda e, o=ycv.ap, a=z.ap[:, 0:N], s=w[:, ch * 3:ch * 3 + 1]: e.tensor_scalar(o, a, s, None, op0=ALU.mult),
                     reads=(z, small), writes=(ycv,))
                P.op("dve", lambda e, o=ycv.ap, a=z.ap[:, 1:N + 1], s=w[:, ch * 3 + 1:ch * 3 + 2], y=ycv.ap: e.scalar_tensor_tensor(out=o, in0=a, scalar=s, in1=y, op0=ALU.mult, op1=ALU.add),
                     reads=(z, small, ycv), writes=(ycv,))
                P.op("dve", lambda e, o=ycv.ap, a=z.ap[:, 2:N + 2], s=w[:, ch * 3 + 2:ch * 3 + 3], y=ycv.ap: e.scalar_tensor_tensor(out=o, in0=a, scalar=s, in1=y, op0=ALU.mult, op1=ALU.add),
                     reads=(z, small, ycv), writes=(ycv,))
                P.op("dve", lambda e, o=yaT.ap[:, ch, :], a=ycv.ap, b=bb.ap: e.tensor_tensor(o, a, b, op=ALU.mult),
                     reads=(ycv, bb), writes=(yaT,))
            P.op("pool", lambda e, o=z.ap[:, 0:2], a=z.ap[:, N:N + 2]: e.tensor_copy(o, a), reads=(z,), writes=(z,))
            if halo:
                P.op("dve", lambda e, o=z.ap[:, 0:2], a=z.ap[:, 0:2], f=small.ap[:, 16:17]: e.tensor_scalar(o, a, f, None, op0=ALU.mult),
                     reads=(z, small), writes=(z,))
        bk = nextI()
        inproj(2048, N, bk)
        P.op("act", lambda e, o=kT[cur].ap[:, slot0 * 128:slot0 * 128 + N], a=bk.ap[:, 0:N]: e.activation(out=o, in_=a, func=AF.Copy),
             reads=(bk,), writes=(kT[cur],))
        bv = nextI()
        for t in range(ntile):
            for kc in range(8):
                mm(P, bv.ap[:, t * 128:(t + 1) * 128], xT.ap[:, kc, t * 128:(t + 1) * 128], Win.ap[:, kc, 2176:2304], kc == 0, kc == 7, (Win, xT), (bv,))
        bv3 = bv.ap[:, 0:N].rearrange("p (t c) -> p t c", t=ntile)
        P.op("dve", lambda e, o=vb[cur].ap[:, slot0:slot0 + ntile, 0, 0:64], a=bv3[:, :, 0:64]: e.tensor_copy(o, a), reads=(bv,), writes=(vb[cur],))
        P.op("dve", lambda e, o=vb[cur].ap[:, slot0:slot0 + ntile, 1, 64:128], a=bv3[:, :, 64:128]: e.tensor_copy(o, a), reads=(bv,), writes=(vb[cur],))
        if halo:
            return
        for j in range(4):
            bq = nextI()
            inproj(1536 + j * 128, N, bq)
            P.op("act", lambda e, o=qT.ap[:, j, :], a=bq.ap: e.activation(out=o, in_=a, func=AF.Copy, scale=0.125), reads=(bq,), writes=(qT,))
        for t in range(4):
            gt = blk * 4 + t
            for kvh in range(2):
                lo, hi = kvh * 64, (kvh + 1) * 64
                for kb in range(2):
                    slot = t + kb
                    S = psS[kvh][kb]
                    mm(P, S.ap.rearrange("p (j q) -> p j q", j=4), kT[cur].ap[lo:hi, slot * 128:(slot + 1) * 128],
                       qT.ap[lo:hi, :, t * 128:(t + 1) * 128], True, True, (kT[cur], qT), (S,))
                    eb_ = eb[kvh][kb]
                    P.op("act", lambda e, o=eb_.ap, a=S.ap: e.activation(out=o, in_=a, func=AF.Exp), reads=(S,), writes=(eb_,))
                    m = maskC if kb == 1 else (maskPF if gt == 0 else maskP)
                    P.op("pool", lambda e, o=eb_.ap, a=eb_.ap, b=m.ap: e.tensor_tensor(o, a, b, op=ALU.mult), reads=(eb_, m), writes=(eb_,))
            i = 0
            for kvh in range(2):
                for kb in range(2):
                    slot = t + kb
                    mm(P, psNum.ap, vb[cur].ap[:, slot, kvh, :], eb[kvh][kb].ap, i == 0, i == 3, (vb[cur], eb[kvh][kb]), (psNum,))
                    i += 1
            i = 0
            for kvh in range(2):
                for kb in range(2):
                    mm(P, psDen.ap, (onesA if kvh == 0 else onesB).ap, eb[kvh][kb].ap, i == 0, i == 3, (onesA, onesB, eb[kvh][kb]), (psDen,))
                    i += 1
            P.op("dve", lambda e, o=dsb.ap, a=psDen.ap, b=esink.ap.rearrange("p j q -> p (j q)"): e.tensor_tensor(o, a, b, op=ALU.add),
                 reads=(psDen, esink), writes=(dsb,))
            P.op("dve", lambda e, o=dsb.ap, a=dsb.ap: e.reciprocal(o, a), reads=(dsb,), writes=(dsb,))
            P.op("dve", lambda e, o=ybT.ap[:, :, t * 128:(t + 1) * 128], a=psNum.ap.rearrange("p (j q) -> p j q", j=4),
                 b=dsb.ap.rearrange("p (j q) -> p j q", j=4): e.tensor_tensor(o, a, b, op=ALU.mult), reads=(psNum, dsb), writes=(ybT,))
            for half in range(2):
                Y = psY[half]
                for c in range(8):
                    src = yaT if c < 4 else ybT
                    mm(P, Y.ap, src.ap[:, c % 4, t * 128:(t + 1) * 128], Wout.ap[:, c, half * 512:(half + 1) * 512], c == 0, c == 7, (src, Wout), (Y,))
            o = xo[gt % 2]
            emit_layernorm(P, C, x, x.ap[:, t, :], psY, gam, bet, o)
            P.dma("sp", dst_dram[gt * 128:(gt + 1) * 128, :], o.ap, reads=(o,), writes=(dst_res[gt],))
        nxt = 1 - cur
        P.op("pool", lambda e, o=kT[nxt].ap[:, 0:128], a=kT[cur].ap[:, 512:640]: e.tensor_copy(o, a), reads=(kT[cur],), writes=(kT[nxt],))
        P.op("pool", lambda e, o=vb[nxt].ap[:, 0, :, :], a=vb[cur].ap[:, 4, :, :]: e.tensor_copy(o, a), reads=(vb[cur],), writes=(vb[nxt],))

    block(-1)
    for blk in range(NT // 4):
        block(blk)


def lay_kmajor(w):
    K, N = w.shape
    return np.ascontiguousarray(w.reshape(K // 128, 128, N).transpose(1, 0, 2).reshape(128, (K // 128) * N))


def rep128(*vecs):
    return np.ascontiguousarray(np.broadcast_to(np.concatenate(vecs)[None, :], (128, sum(v.shape[0] for v in vecs))))


def make_consts():
    ident = np.eye(128, dtype=np.float32)
    k = np.arange(128)[:, None]
    q = np.arange(128)[None, :]
    mc = (k <= q).astype(np.float32)
    mp = (k > q).astype(np.float32)
    onesA = np.zeros((128, 128), np.float32)
    onesA[:, :64] = 1
    onesB = np.zeros((128, 128), np.float32)
    onesB[:, 64:] = 1
    return np.ascontiguousarray(np.concatenate([ident, np.tile(mc, (1, 4)), np.tile(mp, (1, 4)), onesA, onesB], axis=1))


Q_PERM = np.concatenate([np.concatenate([np.arange(j * 64, (j + 1) * 64), np.arange((j + 4) * 64, (j + 5) * 64)]) for j in range(4)])


def even_weights(inp, i, layer):
    w_in = inp["ev_w_in"][i]
    cols = np.concatenate([np.arange(0, 1536), 1536 + Q_PERM, np.arange(2048, 2304)])
    w_in = w_in[:, cols]
    w_out = inp["ev_w_out"][i]
    rows = np.concatenate([np.arange(0, 512), 512 + Q_PERM])
    w_out = w_out[rows, :]
    cw = inp["ev_conv_w"][i]
    convw = cw.reshape(3, 4, 128).transpose(2, 1, 0).reshape(128, 12)
    sk = inp["ev_sinks"][i]
    sinks = np.concatenate([np.broadcast_to(sk[0:4][None], (64, 4)), np.broadcast_to(sk[4:8][None], (64, 4))], axis=0)
    return dict(
        win=lay_kmajor(w_in), wout=lay_kmajor(w_out),
        ln_mix=rep128(inp["ln_mix_g"][layer], inp["ln_mix_b"][layer]),
        small_base=np.concatenate([convw, sinks], axis=1).astype(np.float32),
    )


def ffn_weights(inp, layer):
    return dict(
        wg=lay_kmajor(inp["ffn_w_gate"][layer]), wu=lay_kmajor(inp["ffn_w_up"][layer]), wd=lay_kmajor(inp["ffn_w_down"][layer]),
        ln_ffn=rep128(inp["ln_ffn_g"][layer], inp["ln_ffn_b"][layer]),
    )


ARENA_WORDS = 49 * 1024


def build_even_layer():
    nc = bass.Bass("TRN2", target_bir_lowering=False)
    dt = lambda name, shape, kind="ExternalInput": nc.dram_tensor(name, shape, F32, kind=kind).ap()
    xh = dt("xh", [128, D])
    x = dt("x", [SEG, D])
    win = dt("win", [128, 8 * EVEN_IN])
    wout = dt("wout", [128, 8 * D])
    ln_mix = dt("ln_mix", [128, 2 * D])
    small = dt("small", [128, 17])
    consts = dt("consts", [128, 1408])
    wg = dt("wg", [128, 8 * DFF])
    wu = dt("wu", [128, 8 * DFF])
    wd = dt("wd", [128, NFC * D])
    ln_ffn = dt("ln_ffn", [128, 2 * D])
    x1 = dt("x1", [SEG, D], kind="Internal")
    y = dt("y", [SEG, D], kind="ExternalOutput")
    with contextlib.ExitStack() as es:
        P = Prog(nc, es)
        C = Ctx()
        setup_common(P, C, nc, es, ARENA_WORDS)
        x_res = [Res(f"x{t}") for t in range(NT)]
        x1_res = [Res(f"x1_{t}") for t in range(NT)]
        y_res = [Res(f"y{t}") for t in range(NT)]
        emit_even_mixer_sweep(P, C, xh, x, x_res, x1, x1_res, win, wout, ln_mix, small, consts)
        P.barrier()
        emit_ffn_sweep(P, C, x1, x1_res, y, y_res, wg, wu, wd, ln_ffn, consts[:, 0:128])
        P.finish()
        P.emit()
    return nc


def run_even_layer(nc, xfull, inp, i, layer):
    ew = even_weights(inp, i, layer)
    fw = ffn_weights(inp, layer)
    consts = make_consts()
    in_maps = []
    for c in range(NCORES):
        b, s = divmod(c, 4)
        own = np.ascontiguousarray(xfull[b, s * SEG:(s + 1) * SEG])
        halo = np.ascontiguousarray(xfull[b, s * SEG - 128:s * SEG]) if s > 0 else np.zeros((128, D), np.float32)
        flag = np.full((128, 1), 1.0 if s > 0 else 0.0, np.float32)
        in_maps.append(dict(xh=halo, x=own, win=ew["win"], wout=ew["wout"], ln_mix=ew["ln_mix"],
                            small=np.ascontiguousarray(np.concatenate([ew["small_base"], flag], axis=1)),
                            consts=consts, wg=fw["wg"], wu=fw["wu"], wd=fw["wd"], ln_ffn=fw["ln_ffn"]))
    res = run_bass_kernel_spmd(nc, in_maps, core_ids=list(range(NCORES)))
    out = np.empty_like(xfull)
    for c in range(NCORES):
        b, s = divmod(c, 4)
        out[b, s * SEG:(s + 1) * SEG] = res.results[c]["y"]
    return out


W1 = 336
W2 = 704
SCALE_MLA = 96 ** -0.5
NTS = SEQ // 128
ODD_NBLK = SEQ // 512


def emit_odd_mixer(P, C, x_d, out_d, wfeat_d, wtok_d, wq_d, wkv_d, wgate_d, gvec_d, rope_d, consts_d):
    A = C.arena
    A.off = 0
    Wf = A.bf16("Wf", 8, W1)
    Wt = A.bf16("Wt", 8, W2)
    Wq = A.bf16("Wq", 2, 192)
    Wkv = A.bf16("Wkv", 224)
    Wga = A.bf16("Wga", 64)
    KT = A.bf16("KT", SEQ)
    Va = A.bf16("Va", NTS, 130)
    xT = A.bf16("xT", 8, 512)
    cqT = A.bf16("cqT", 2, 512)
    cnT = A.bf16("cnT", 512)
    QT = A.bf16("QT", 512)
    glT = A.bf16("glT", 512)
    QD = A.bf16("QD", 2, 128)
    kdT = A.bf16("kdT", 128)
    kdec = A.bf16("kdec", 64)
    vtk = A.bf16("vtk", 128)
    attn = A.bf16("attn", 128)
    Sbf = [A.bf16(f"Sbf{i}", 128) for i in range(2)]
    eb = [A.bf16(f"eb{i}", 512) for i in range(2)]
    C.ident = A.f32("ident", 128)
    LT2 = A.f32("LT2", 128)
    ON2 = A.f32("ON2", 128)
    mBD = A.f32("mBD", 128)
    tri = A.bf16("tri", 128)
    gv = A.f32("gvec", 512)
    xt = [A.f32(f"xt{i}", 4, 1024) for i in range(2)]
    cs = [A.f32(f"cs{i}", 2, 512) for i in range(2)]
    rtmp = A.f32("rtmp", 2, 512)
    lsp = A.f32("lsp", 64)
    btk = A.f32("btk", 64)
    dif = A.f32("dif", 64)
    E1 = A.f32("E1", 128)
    E2 = A.f32("E2", 128)
    S = A.f32("S", 128)
    sr = A.f32("sr", 128)
    cq = A.f32("cq", 384)
    sm = A.f32("sm", 16)
    og = A.f32("og", 128)
    oo = [A.f32(f"oo{i}", 256) for i in range(4)]
    junk = A.f32("junk", 384)

    bk = C.bank
    ps = C.ps_all

    def sub(bank, a, b, name):
        return SubRes(name, ps[:, bank * 512 + a: bank * 512 + b], bk[bank])
    pF = bk[0]
    pKV3 = sub(2, 0, 320, "pKVR")
    pZ = sub(2, 320, 384, "pZ")
    pBt = sub(2, 384, 448, "pBt")
    pBl = sub(2, 448, 512, "pBl")
    pC = sub(3, 0, 384, "pC")
    pKVs = sub(3, 384, 512, "pKVs")
    pBT = sub(4, 0, 128, "pBT")
    pAt = sub(4, 128, 256, "pAt")
    pO = sub(4, 256, 384, "pO")
    pTr = sub(4, 384, 512, "pTr")
    pS = [bk[5], bk[6]]
    pAcc = [bk[1], bk[7]]

    P.dma("sp", C.ident.ap, consts_d[:, 0:128], writes=(C.ident,))
    P.dma("sp", LT2.ap, consts_d[:, 128:256], writes=(LT2,))
    P.dma("sp", ON2.ap, consts_d[:, 256:384], writes=(ON2,))
    P.dma("sp", mBD.ap, consts_d[:, 384:512], writes=(mBD,))
    P.dma("sp", gv.ap, gvec_d, writes=(gv,))
    P.dma("pool", tri.ap, consts_d[:, 512:640], writes=(tri,))
    P.dma("pool", Wf.ap, wfeat_d.rearrange("p (k n) -> p k n", k=8), writes=(Wf,))
    P.dma("pool", Wt.ap, wtok_d.rearrange("p (k n) -> p k n", k=8), writes=(Wt,))
    P.dma("pool", Wq.ap, wq_d.rearrange("p (k n) -> p k n", k=2), writes=(Wq,))
    P.dma("pool", Wkv.ap, wkv_d, writes=(Wkv,))
    P.dma("pool", Wga.ap, wgate_d, writes=(Wga,))
    P.op("pool", lambda e, o=QD.ap: e.memset(o, 0.0), writes=(QD,))
    P.op("pool", lambda e, o=S.ap: e.memset(o, 0.0), writes=(S,))
    P.op("pool", lambda e, o=Sbf[0].ap: e.memset(o, 0.0), writes=(Sbf[0],))
    P.op("pool", lambda e, o=glT.ap: e.memset(o, 1.0), writes=(glT,))
    P.op("pool", lambda e, o=Va.ap: e.memset(o, 1.0), writes=(Va,))

    def load_x(b):
        P.dma("sp", xt[b % 2].ap, x_d[b * 512:(b + 1) * 512, :].rearrange("(t p) d -> p t d", p=128), writes=(xt[b % 2],))
        P.dma("sp", cs[b % 2].ap[64:96, 0, :], rope_d[:, b * 512:(b + 1) * 512], writes=(cs[b % 2],))
        P.dma("sp", cs[b % 2].ap[64:96, 1, :], rope_d[:, SEQ + b * 512:SEQ + (b + 1) * 512], writes=(cs[b % 2],))

    def featproj(c0, M, out_ap, writes, start=True, stop=True):
        for kc in range(8):
            mm(P, out_ap, Wf.ap[:, kc, c0:c0 + M], xT.ap[:, kc, :], start and kc == 0, stop and kc == 7, (Wf, xT), writes)

    def rms_rstd(src_ap, src_res, n, col):
        P.op("act", lambda e, o=junk.ap[:, 0:n], a=src_ap, acc=sm.ap[:, col:col + 1]: e.activation(out=o, in_=a, func=AF.Square, accum_out=acc),
             reads=(src_res,), writes=(junk, sm))
        P.op("act", lambda e, o=sm.ap[:, col:col + 1], a=sm.ap[:, col:col + 1]: e.activation(out=o, in_=a, func=AF.Ln, bias=RMS_EPS, scale=1.0 / n),
             reads=(sm,), writes=(sm,))
        P.op("act", lambda e, o=sm.ap[:, col:col + 1], a=sm.ap[:, col:col + 1]: e.activation(out=o, in_=a, func=AF.Exp, scale=-0.5),
             reads=(sm,), writes=(sm,))

    load_x(0)
    for blk in range(ODD_NBLK):
        x = xt[blk % 2]
        csb = cs[blk % 2]
        C.psT = [pF]
        C.psT_i = 0
        emit_transposes(P, C, x, 4, xT, evac_flip=blk)
        if blk + 1 < ODD_NBLK:
            load_x(blk + 1)
        featproj(128, 32, pF.ap[0:32, :], (pF,))
        P.op("act", lambda e, o=glT.ap[0:16, :], a=pF.ap[0:16, :]: e.activation(out=o, in_=a, func=AF.Copy), reads=(pF,), writes=(glT,))
        for t in range(4):
            gt = blk * 4 + t
            tok = slice(t * 128, (t + 1) * 128)
            for kc in range(8):
                mm(P, pKV3.ap, xT.ap[:, kc, tok], Wt.ap[:, kc, 0:320], kc == 0, kc == 7, (Wt, xT), (pKV3,))
            for kc in range(8):
                mm(P, pC.ap, xT.ap[:, kc, tok], Wt.ap[:, kc, 320:704], kc == 0, kc == 7, (Wt, xT), (pC,))
            mm(P, pZ.ap, glT.ap[0:32, tok], Wga.ap[0:32, :], True, True, (glT, Wga), (pZ,))
            P.op("act", lambda e, o=lsp.ap, a=pZ.ap: e.activation(out=o, in_=a, func=AF.Exp, scale=-1.0), reads=(pZ,), writes=(lsp,))
            P.op("act", lambda e, o=lsp.ap, a=lsp.ap: e.activation(out=o, in_=a, func=AF.Ln, bias=1.0), reads=(lsp,), writes=(lsp,))
            mm(P, pBt.ap, LT2.ap, lsp.ap, True, True, (LT2, lsp), (pBt,))
            mm(P, pBl.ap, ON2.ap, lsp.ap, True, True, (ON2, lsp), (pBl,))
            mm(P, pBT.ap[0:64, :], lsp.ap, LT2.ap, True, True, (LT2, lsp), (pBT,))
            P.op("act", lambda e, o=E1.ap[0:64, :], a=pBT.ap[0:64, :]: e.activation(out=o, in_=a, func=AF.Exp), reads=(pBT,), writes=(E1,))
            P.op("act", lambda e, o=E2.ap[0:64, :], a=pBT.ap[0:64, :]: e.activation(out=o, in_=a, func=AF.Exp, scale=-1.0), reads=(pBT,), writes=(E2,))
            P.op("act", lambda e, o=btk.ap, a=pBt.ap: e.activation(out=o, in_=a, func=AF.Copy), reads=(pBt,), writes=(btk,))
            P.op("dve", lambda e, o=dif.ap, a=pBl.ap, b=btk.ap: e.tensor_tensor(o, a, b, op=ALU.subtract), reads=(pBl, btk), writes=(dif,))
            P.op("act", lambda e, o=dif.ap, a=dif.ap: e.activation(out=o, in_=a, func=AF.Exp), reads=(dif,), writes=(dif,))
            P.op("dve", lambda e, o=kdec.ap, a=pKV3.ap[:, 0:64], b=dif.ap: e.tensor_tensor(o, a, b, op=ALU.mult), reads=(pKV3, dif), writes=(kdec,))
            P.op("act", lambda e, o=vtk.ap, a=pKV3.ap[:, 64:192]: e.activation(out=o, in_=a, func=AF.Copy), reads=(pKV3,), writes=(vtk,))
            P.op("act", lambda e, o=sr.ap, a=pKV3.ap[:, 192:320]: e.activation(out=o, in_=a, func=AF.Exp, scale=-1.0), reads=(pKV3,), writes=(sr,))
            P.op("dve", lambda e, o=sr.ap, a=sr.ap: e.tensor_scalar(o, a, 1.0, None, op0=ALU.add), reads=(sr,), writes=(sr,))
            P.op("dve", lambda e, o=sr.ap, a=sr.ap: e.reciprocal(o, a), reads=(sr,), writes=(sr,))
            P.op("dve", lambda e, o=sr.ap, a=pKV3.ap[:, 192:320], b=sr.ap: e.tensor_tensor(o, a, b, op=ALU.mult), reads=(pKV3, sr), writes=(sr,))
            for kc in range(8):
                mm(P, pF.ap[0:64, 0:128], Wf.ap[:, kc, 0:64], xT.ap[:, kc, tok], kc == 0, kc == 7, (Wf, xT), (pF,))
            for kc in range(8):
                mm(P, pF.ap[0:64, 128:256], Wf.ap[:, kc, 64:128], xT.ap[:, kc, tok], kc == 0, kc == 7, (Wf, xT), (pF,))
            P.op("dve", lambda e, o=QD.ap[0:64, 0, 0:64], a=pF.ap[0:64, 0:64], b=E1.ap[0:64, 0:64]: e.scalar_tensor_tensor(out=o, in0=a, scalar=0.125, in1=b, op0=ALU.mult, op1=ALU.mult),
                 reads=(pF, E1), writes=(QD,))
            P.op("dve", lambda e, o=QD.ap[0:64, 1, 64:128], a=pF.ap[0:64, 64:128], b=E1.ap[0:64, 64:128]: e.scalar_tensor_tensor(out=o, in0=a, scalar=0.125, in1=b, op0=ALU.mult, op1=ALU.mult),
                 reads=(pF, E1), writes=(QD,))
            P.op("dve", lambda e, o=kdT.ap[0:64, :], a=pF.ap[0:64, 128:256], b=E2.ap[0:64, :]: e.tensor_tensor(o, a, b, op=ALU.mult), reads=(pF, E2), writes=(kdT,))
            mm(P, pAt.ap, kdT.ap[0:64, :], QD.ap[0:64, 0, :], True, False, (kdT, QD), (pAt,))
            mm(P, pAt.ap, kdT.ap[0:64, :], QD.ap[0:64, 1, :], False, True, (kdT, QD), (pAt,))
            P.op("dve", lambda e, o=attn.ap, a=pAt.ap, b=mBD.ap: e.tensor_tensor(o, a, b, op=ALU.mult), reads=(pAt, mBD), writes=(attn,))
            s0 = Sbf[0]
            s1 = Sbf[1]
            mm(P, pO.ap, attn.ap, vtk.ap, True, False, (attn, vtk), (pO,))
            mm(P, pO.ap, QD.ap[0:64, 0, :], s0.ap[0:64, :], False, False, (QD, s0), (pO,))
            mm(P, pKVs.ap[0:64, :], kdec.ap[0:64, :], vtk.ap[0:64, :], True, True, (kdec, vtk), (pKVs,))
            P.op("dve", lambda e, o=S.ap[0:64, :], a=S.ap[0:64, :], d=E1.ap[0:64, 63:64], kv=pKVs.ap[0:64, :]: e.scalar_tensor_tensor(out=o, in0=a, scalar=d, in1=kv, op0=ALU.mult, op1=ALU.add),
                 reads=(S, E1, pKVs), writes=(S,))
            P.op("act", lambda e, o=s1.ap[0:64, :], a=S.ap[0:64, :]: e.activation(out=o, in_=a, func=AF.Copy), reads=(S,), writes=(s1,))
            mm(P, pO.ap, QD.ap[0:64, 1, :], s1.ap[0:64, :], False, True, (QD, s1), (pO,))
            mm(P, pKVs.ap[0:64, :], kdec.ap[64:128, :], vtk.ap[64:128, :], True, True, (kdec, vtk), (pKVs,))
            P.op("dve", lambda e, o=S.ap[0:64, :], a=S.ap[0:64, :], d=E1.ap[0:64, 127:128], kv=pKVs.ap[0:64, :]: e.scalar_tensor_tensor(out=o, in0=a, scalar=d, in1=kv, op0=ALU.mult, op1=ALU.add),
                 reads=(S, E1, pKVs), writes=(S,))
            P.op("act", lambda e, o=s0.ap[0:64, :], a=S.ap[0:64, :]: e.activation(out=o, in_=a, func=AF.Copy), reads=(S,), writes=(s0,))
            ob = oo[gt % 4]
            rms_rstd(pO.ap, pO, 128, 0)
            P.op("dve", lambda e, o=og.ap, a=pO.ap, s=sm.ap[:, 0:1], g=gv.ap[:, 384:512]: e.scalar_tensor_tensor(out=o, in0=a, scalar=s, in1=g, op0=ALU.mult, op1=ALU.mult),
                 reads=(pO, sm, gv), writes=(og,))
            P.op("pool", lambda e, o=ob.ap[:, 0:128], a=og.ap, b=sr.ap: e.tensor_tensor(o, a, b, op=ALU.mult), reads=(og, sr), writes=(ob,))
            rms_rstd(pC.ap[:, 0:256], pC, 256, 1)
            rms_rstd(pC.ap[:, 256:384], pC, 128, 2)
            P.op("dve", lambda e, o=cq.ap[:, 0:256], a=pC.ap[:, 0:256], s=sm.ap[:, 1:2], g=gv.ap[:, 0:256]: e.scalar_tensor_tensor(out=o, in0=a, scalar=s, in1=g, op0=ALU.mult, op1=ALU.mult),
                 reads=(pC, sm, gv), writes=(cq,))
            P.op("dve", lambda e, o=cq.ap[:, 256:384], a=pC.ap[:, 256:384], s=sm.ap[:, 2:3], g=gv.ap[:, 256:384]: e.scalar_tensor_tensor(out=o, in0=a, scalar=s, in1=g, op0=ALU.mult, op1=ALU.mult),
                 reads=(pC, sm, gv), writes=(cq,))
            for j in range(3):
                P.op("pe", lambda e, o=pTr.ap, a=cq.ap[:, j * 128:(j + 1) * 128], idn=C.ident.ap: e.transpose(o, a, idn), reads=(cq, C.ident), writes=(pTr,))
                dst = cqT.ap[:, j, tok] if j < 2 else cnT.ap[:, tok]
                dres = cqT if j < 2 else cnT
                P.op("act", lambda e, o=dst, a=pTr.ap: e.activation(out=o, in_=a, func=AF.Copy), reads=(pTr,), writes=(dres,))
            mm(P, pTr.ap, cnT.ap[:, tok], Wkv.ap[:, 96:224], True, True, (cnT, Wkv), (pTr,))
            P.op("act", lambda e, o=Va.ap[:, gt, 0:128], a=pTr.ap: e.activation(out=o, in_=a, func=AF.Copy), reads=(pTr,), writes=(Va,))
        mm(P, pF.ap[0:96, :], Wkv.ap[:, 0:96], cnT.ap, True, False, (Wkv, cnT), (pF,))
        featproj(144, 96, pF.ap[0:96, :], (pF,), start=False, stop=True)
        P.op("act", lambda e, o=KT.ap[0:64, blk * 512:(blk + 1) * 512], a=pF.ap[0:64, :]: e.activation(out=o, in_=a, func=AF.Copy), reads=(pF,), writes=(KT,))
        P.op("dve", lambda e, o=rtmp.ap[64:96, 0, :], a=pF.ap[64:96, :], b=csb.ap[64:96, 0, :]: e.tensor_tensor(o, a, b, op=ALU.mult), reads=(pF, csb), writes=(rtmp,))
        featproj(240, 96, pF.ap[0:96, :], (pF,))
        P.op("dve", lambda e, o=rtmp.ap[64:96, 1, :], a=pF.ap[64:96, :], b=csb.ap[64:96, 1, :]: e.tensor_tensor(o, a, b, op=ALU.mult), reads=(pF, csb), writes=(rtmp,))
        P.op("dve", lambda e, o=KT.ap[64:96, blk * 512:(blk + 1) * 512], a=rtmp.ap[64:96, 0, :], b=rtmp.ap[64:96, 1, :]: e.tensor_tensor(o, a, b, op=ALU.add), reads=(rtmp,), writes=(KT,))
        for c in range(2):
            mm(P, pF.ap[0:96, :], Wq.ap[:, c, 0:96], cqT.ap[:, c, :], c == 0, c == 1, (Wq, cqT), (pF,))
        P.op("act", lambda e, o=QT.ap[0:64, :], a=pF.ap[0:64, :]: e.activation(out=o, in_=a, func=AF.Copy), reads=(pF,), writes=(QT,))
        P.op("dve", lambda e, o=rtmp.ap[64:96, 0, :], a=pF.ap[64:96, :], b=csb.ap[64:96, 0, :]: e.tensor_tensor(o, a, b, op=ALU.mult), reads=(pF, csb), writes=(rtmp,))
        for c in range(2):
            mm(P, pF.ap[0:96, :], Wq.ap[:, c, 96:192], cqT.ap[:, c, :], c == 0, c == 1, (Wq, cqT), (pF,))
        P.op("dve", lambda e, o=rtmp.ap[64:96, 1, :], a=pF.ap[64:96, :], b=csb.ap[64:96, 1, :]: e.tensor_tensor(o, a, b, op=ALU.mult), reads=(pF, csb), writes=(rtmp,))
        P.op("dve", lambda e, o=QT.ap[64:96, :], a=rtmp.ap[64:96, 0, :], b=rtmp.ap[64:96, 1, :]: e.tensor_tensor(o, a, b, op=ALU.add), reads=(rtmp,), writes=(QT,))
        nk = 4 * blk + 4
        for j in range(nk):
            Sb = pS[j % 2]
            e_ = eb[j % 2]
            jj = j - 4 * blk
            mm(P, Sb.ap, KT.ap[0:96, j * 128:(j + 1) * 128], QT.ap[0:96, :], True, True, (KT, QT), (Sb,))
            q0 = max(jj, 0)
            P.op("act", lambda e, o=e_.ap[:, q0 * 128:512], a=Sb.ap[:, q0 * 128:512]: e.activation(out=o, in_=a, func=AF.Exp, scale=SCALE_MLA),
                 reads=(Sb,), writes=(e_,))
            if jj >= 0:
                P.op("pool", lambda e, o=e_.ap[:, jj * 128:(jj + 1) * 128], a=e_.ap[:, jj * 128:(jj + 1) * 128], b=tri.ap: e.tensor_tensor(o, a, b, op=ALU.mult),
                     reads=(e_, tri), writes=(e_,))
            for t in range(q0, 4):
                acc = pAcc[t // 2]
                c0 = (t % 2) * 130
                mm(P, acc.ap[:, c0:c0 + 130], e_.ap[:, t * 128:(t + 1) * 128], Va.ap[:, j, 0:130], j == 0 and t % 2 == 0, j == 4 * blk + t, (e_, Va), (acc,), skip=True)
        for t in range(4):
            gt = blk * 4 + t
            acc = pAcc[t // 2]
            c0 = (t % 2) * 130
            ob = oo[gt % 4]
            P.op("dve", lambda e, o=sm.ap[:, 4 + t:5 + t], a=acc.ap[:, c0 + 128:c0 + 129]: e.reciprocal(o, a), reads=(acc,), writes=(sm,))
            P.op("dve", lambda e, o=ob.ap[:, 128:256], a=acc.ap[:, c0:c0 + 128], s=sm.ap[:, 4 + t:5 + t]: e.tensor_scalar(o, a, s, None, op0=ALU.mult),
                 reads=(acc, sm), writes=(ob,))
            P.dma("sp", out_d[gt * 128:(gt + 1) * 128, :], ob.ap, reads=(ob,))


def odd_mixer_weights(inp, i, h):
    w_in = inp["od_w_in"][i]
    z64 = np.zeros((D, 64), np.float32)
    kr = w_in[:, 1936:1968]
    kr_sw = np.concatenate([kr[:, 16:32], kr[:, 0:16]], axis=1)
    wfeat = np.concatenate([w_in[:, h * 64:(h + 1) * 64], w_in[:, 256 + h * 64:256 + (h + 1) * 64], w_in[:, 1024:1040],
                            z64, kr, z64, kr_sw], axis=1)
    wtok = np.concatenate([w_in[:, 256 + h * 64:256 + (h + 1) * 64], w_in[:, 512 + h * 128:512 + (h + 1) * 128],
                           w_in[:, 1040 + h * 128:1040 + (h + 1) * 128], w_in[:, 1552:1808], w_in[:, 1808:1936]], axis=1)
    wuq = inp["od_mla_w_uq"][i][:, h * 96:(h + 1) * 96]
    rp = wuq[:, 64:96]
    rp_sw = np.concatenate([rp[:, 16:32], rp[:, 0:16]], axis=1)
    z256 = np.zeros((256, 64), np.float32)
    wq = np.concatenate([wuq, z256, rp_sw], axis=1)
    wukv = inp["od_mla_w_ukv"][i][:, h * 192:(h + 1) * 192]
    wkv = np.concatenate([wukv[:, 0:64], np.zeros((128, 32), np.float32), wukv[:, 64:192]], axis=1)
    wg = np.zeros((128, 64), np.float32)
    wg[0:16] = inp["od_gla_w_gate"][i][:, h * 64:(h + 1) * 64]
    wg[16] = inp["od_gla_b_gate"][i][h * 64:(h + 1) * 64]
    return dict(wfeat=lay_kmajor(wfeat), wtok=lay_kmajor(wtok), wq=lay_kmajor(wq), wkv=np.ascontiguousarray(wkv), wgate=wg,
                gvec=rep128(inp["od_mla_q_norm_g"][i], inp["od_mla_kv_norm_g"][i], inp["od_gla_norm_g"][i]))


def odd_consts():
    ident = np.eye(128, dtype=np.float32)
    s = np.arange(128)[:, None]
    t = np.arange(128)[None, :]
    same = (s // 64) == (t // 64)
    lt2 = (same & (s <= t)).astype(np.float32) * np.float32(-1.0 / 16.0)
    on2 = same.astype(np.float32) * np.float32(-1.0 / 16.0)
    mbd = (same & (s <= t)).astype(np.float32)
    tri = (s <= t).astype(np.float32)
    pos = np.arange(SEQ, dtype=np.float32)
    inv_freq = (np.float32(10000.0) ** (-np.arange(0, 32, 2, dtype=np.float32) / np.float32(32))).astype(np.float32)
    ang = (pos[:, None] * inv_freq[None, :]).astype(np.float32)
    cos, sin = np.cos(ang).astype(np.float32), np.sin(ang).astype(np.float32)
    rope = np.concatenate([np.concatenate([cos, cos], axis=1).T, np.concatenate([-sin, sin], axis=1).T], axis=1)
    return np.ascontiguousarray(np.concatenate([ident, lt2, on2, mbd, tri], axis=1)), np.ascontiguousarray(rope.astype(np.float32))


def build_odd_mixer():
    nc = bass.Bass("TRN2", target_bir_lowering=False)
    dt = lambda name, shape, kind="ExternalInput": nc.dram_tensor(name, shape, F32, kind=kind).ap()
    x = dt("x", [ODD_NBLK * 512, D])
    wfeat = dt("wfeat", [128, 8 * W1])
    wtok = dt("wtok", [128, 8 * W2])
    wq = dt("wq", [128, 2 * 192])
    wkv = dt("wkv", [128, 224])
    wgate = dt("wgate", [128, 64])
    gvec = dt("gvec", [128, 512])
    rope = dt("rope", [32, 2 * SEQ])
    consts = dt("consts", [128, 640])
    out = dt("out", [ODD_NBLK * 512, 256], kind="ExternalOutput")
    with contextlib.ExitStack() as es:
        P = Prog(nc, es)
        C = Ctx()
        setup_common(P, C, nc, es, ARENA_WORDS)
        emit_odd_mixer(P, C, x, out, wfeat, wtok, wq, wkv, wgate, gvec, rope, consts)
        P.finish()
        P.emit()
    return nc


def run_odd_mixer(nc, xfull, inp, i):
    consts, rope = odd_consts()
    in_maps = []
    for c in range(NCORES):
        b, h = divmod(c, 4)
        w = odd_mixer_weights(inp, i, h)
        in_maps.append(dict(x=np.ascontiguousarray(xfull[b][:ODD_NBLK * 512]), rope=rope, consts=consts, **w))
    res = run_bass_kernel_spmd(nc, in_maps, core_ids=list(range(NCORES)))
    mix = np.zeros((2, SEQ, D), np.float32)
    n = ODD_NBLK * 512
    for c in range(NCORES):
        b, h = divmod(c, 4)
        o = res.results[c]["out"]
        mix[b, :n, h * 128:(h + 1) * 128] = o[:, 0:128]
        mix[b, :n, 512 + h * 128:512 + (h + 1) * 128] = o[:, 128:256]
    return mix


def build_tail(with_even):
    nt = NT + 1 if with_even else NT
    nc = bass.Bass("TRN2", target_bir_lowering=False)
    dt = lambda name, shape, kind="ExternalInput": nc.dram_tensor(name, shape, F32, kind=kind).ap()
    mix = dt("mix", [nt * 128, D])
    xr = dt("xr", [nt * 128, D])
    o_wout = dt("o_wout", [128, 8 * D])
    o_ln_mix = dt("o_ln_mix", [128, 2 * D])
    o_wg = dt("o_wg", [128, 8 * DFF])
    o_wu = dt("o_wu", [128, 8 * DFF])
    o_wd = dt("o_wd", [128, NFC * D])
    o_ln_ffn = dt("o_ln_ffn", [128, 2 * D])
    consts = dt("consts", [128, 1408])
    xa = dt("xa", [nt * 128, D], kind="Internal")
    y = dt("y", [SEG, D], kind="ExternalOutput")
    if with_even:
        xb = dt("xb", [nt * 128, D], kind="Internal")
        xc = dt("xc", [SEG, D], kind="Internal")
        win = dt("win", [128, 8 * EVEN_IN])
        wout = dt("wout", [128, 8 * D])
        ln_mix = dt("ln_mix", [128, 2 * D])
        small = dt("small", [128, 17])
        wg = dt("wg", [128, 8 * DFF])
        wu = dt("wu", [128, 8 * DFF])
        wd = dt("wd", [128, NFC * D])
        ln_ffn = dt("ln_ffn", [128, 2 * D])
    with contextlib.ExitStack() as es:
        P = Prog(nc, es)
        C = Ctx()
        setup_common(P, C, nc, es, ARENA_WORDS)
        xa_res = [Res(f"xa{t}") for t in range(nt)]
        emit_tail_sweep(P, C, mix, xr, xa, xa_res, o_wout, o_ln_mix, consts[:, 0:128], nt)
        P.barrier()
        if not with_even:
            y_res = [Res(f"y{t}") for t in range(NT)]
            emit_ffn_sweep(P, C, xa, xa_res, y, y_res, o_wg, o_wu, o_wd, o_ln_ffn, consts[:, 0:128], ntiles=nt)
        else:
            xb_res = [Res(f"xb{t}") for t in range(nt)]
            emit_ffn_sweep(P, C, xa, xa_res, xb, xb_res, o_wg, o_wu, o_wd, o_ln_ffn, consts[:, 0:128], ntiles=nt)
            P.barrier()
            xc_res = [Res(f"xc{t}") for t in range(NT)]
            y_res = [Res(f"y{t}") for t in range(NT)]
            emit_even_mixer_sweep(P, C, xb[0:128, :], xb[128:, :], xb_res[1:], xc, xc_res, win, wout, ln_mix, small, consts)
            P.barrier()
            emit_ffn_sweep(P, C, xc, xc_res, y, y_res, wg, wu, wd, ln_ffn, consts[:, 0:128])
        P.finish()
        P.emit()
    return nc


def run_tail(nc, mixfull, xfull, inp, i_odd, layer_odd, with_even, i_even=None, layer_even=None):
    consts = make_consts()
    ow = dict(o_wout=lay_kmajor(inp["od_w_out"][i_odd]),
              o_ln_mix=rep128(inp["ln_mix_g"][layer_odd], inp["ln_mix_b"][layer_odd]))
    fo = ffn_weights(inp, layer_odd)
    ow.update(o_wg=fo["wg"], o_wu=fo["wu"], o_wd=fo["wd"], o_ln_ffn=fo["ln_ffn"])
    if with_even:
        ew = even_weights(inp, i_even, layer_even)
        fw = ffn_weights(inp, layer_even)
    in_maps = []
    for c in range(NCORES):
        b, sg_ = divmod(c, 4)
        lo = sg_ * SEG
        if with_even:
            if sg_ > 0:
                m = np.ascontiguousarray(mixfull[b, lo - 128:lo + SEG])
                xx = np.ascontiguousarray(xfull[b, lo - 128:lo + SEG])
            else:
                m = np.concatenate([np.zeros((128, D), np.float32), mixfull[b, lo:lo + SEG]], axis=0)
                xx = np.concatenate([np.zeros((128, D), np.float32), xfull[b, lo:lo + SEG]], axis=0)
            flag = np.full((128, 1), 1.0 if sg_ > 0 else 0.0, np.float32)
            d = dict(mix=m, xr=xx, consts=consts, win=ew["win"], wout=ew["wout"], ln_mix=ew["ln_mix"],
                     small=np.ascontiguousarray(np.concatenate([ew["small_base"], flag], axis=1)),
                     wg=fw["wg"], wu=fw["wu"], wd=fw["wd"], ln_ffn=fw["ln_ffn"], **ow)
        else:
            d = dict(mix=np.ascontiguousarray(mixfull[b, lo:lo + SEG]), xr=np.ascontiguousarray(xfull[b, lo:lo + SEG]), consts=consts, **ow)
        in_maps.append(d)
    res = run_bass_kernel_spmd(nc, in_maps, core_ids=list(range(NCORES)))
    out = np.empty_like(xfull)
    for c in range(NCORES):
        b, sg_ = divmod(c, 4)
        out[b, sg_ * SEG:(sg_ + 1) * SEG] = res.results[c]["y"]
    return out


def kernel(**inputs):
    inp = {k: np.asarray(v) for k, v in inputs.items()}
    x0 = np.ascontiguousarray(inp["x"], dtype=np.float32)
    nc_even = build_even_layer()
    x1 = run_even_layer(nc_even, x0, inp, 0, 0)
    nc_odd = build_odd_mixer()
    mix1 = run_odd_mixer(nc_odd, x1, inp, 0)
    nc_te = build_tail(True)
    x3 = run_tail(nc_te, mix1, x1, inp, 0, 1, True, 1, 2)
    mix3 = run_odd_mixer(nc_odd, x3, inp, 1)
    nc_t = build_tail(False)
    x4 = run_tail(nc_t, mix3, x3, inp, 1, 3, False)
    return x4.astype(np.float32)
```

```python
import contextlib
import numpy as np
import ml_dtypes
import concourse.bass as bass
import concourse.mybir as mybir
from concourse.bass_utils import run_bass_kernel_spmd

F32 = mybir.dt.float32
BF16 = mybir.dt.bfloat16
AF = mybir.ActivationFunctionType
ALU = mybir.AluOpType

D = 1024
NCORES = 8
SEQ = 16384
SEG = 4096
NT = SEG // 128
DFF = 2816
NFC = DFF // 128
DEPTH = 4
ALPHA = (2.0 * DEPTH) ** 0.25
LN_EPS = 1e-5
RMS_EPS = 1e-6
EVEN_IN = 2304
ODD_IN = 1968


class Res:
    __slots__ = ("name", "w", "r", "ap")

    def __init__(self, name, ap=None):
        self.name = name
        self.w = None
        self.r = {}
        self.ap = ap


class BankRes(Res):
    __slots__ = ("idx",)


class SubRes:
    __slots__ = ("name", "ap", "parent")

    def __init__(self, name, ap, parent):
        self.name = name
        self.ap = ap
        self.parent = parent


class Prog:
    ENG = ("pe", "act", "dve", "pool", "sp")

    def __init__(self, nc, es, n_dma_sems=40):
        self.nc = nc
        self.semh = {}
        for e in ("pe", "act", "dve", "pool"):
            self.semh["s_" + e] = es.enter_context(nc.semaphore("s_" + e))
        self.cnt = {e: 0 for e in ("pe", "act", "dve", "pool")}
        self.ops = {e: [] for e in self.ENG}
        self.seen = {e: {} for e in self.ENG}
        self.nd = n_dma_sems
        for i in range(n_dma_sems):
            self.semh[f"d{i}"] = es.enter_context(nc.semaphore(f"d{i}"))
        self.dval = [0] * n_dma_sems
        self.dnext = 0
        self.uid = 0
        self.nops = 0
        self.after_op = None

    def _wait(self, eng, tok):
        if tok is None:
            return
        name, val = tok
        if eng == "pe" and name == "s_pe":
            return
        if self.seen[eng].get(name, 0) >= val:
            return
        self.seen[eng][name] = val
        self.ops[eng].append(("w", name, val))

    def _deps(self, eng, reads, writes):
        for r in reads:
            if r.w is not None:
                self._wait(eng, r.w)
        for w in writes:
            if w.w is not None:
                self._wait(eng, w.w)
            for t in w.r.values():
                self._wait(eng, t)

    @staticmethod
    def _norm(reads, writes):
        reads = [getattr(r, "parent", None) or r for r in reads]
        writes = [getattr(w, "parent", None) or w for w in writes]
        for r in reads:
            if isinstance(r, BankRes) and r not in writes:
                writes.append(r)
        return reads, writes

    def op(self, eng, fn, reads=(), writes=()):
        reads, writes = self._norm(reads, writes)
        self._deps(eng, reads, writes)
        self.cnt[eng] += 1
        tok = ("s_" + eng, self.cnt[eng])
        self.ops[eng].append(("o", fn, "s_" + eng, 1))
        for r in reads:
            r.r[eng] = tok
        for w in writes:
            w.w = tok
            w.r = {}
        self.nops += 1
        if self.after_op is not None:
            h = self.after_op
            self.after_op = None
            h()
            self.after_op = h
        return tok

    def dma(self, queue, out, in_, reads=(), writes=()):
        i = self.dnext
        self.dnext = (i + 1) % self.nd
        name = f"d{i}"
        if self.dval[i] > 0:
            self._wait(queue, (name, self.dval[i]))
        self._deps(queue, reads, writes)
        self.dval[i] += 16
        tok = (name, self.dval[i])
        self.ops[queue].append(("o", lambda e, o=out, a=in_: e.dma_start(out=o, in_=a), name, 16))
        for r in reads:
            self.uid += 1
            r.r[("dma", self.uid)] = tok
        for w in writes:
            w.w = tok
            w.r = {}
        self.nops += 1
        return tok

    def barrier(self):
        toks = [("s_" + e, self.cnt[e]) for e in ("pe", "act", "dve", "pool") if self.cnt[e] > 0]
        toks += [(f"d{i}", self.dval[i]) for i in range(self.nd) if self.dval[i] > 0]
        for e in self.ENG:
            for t in toks:
                if e != "sp" and t[0] == "s_" + e:
                    continue
                self._wait(e, t)

    def finish(self):
        for i in range(self.nd):
            if self.dval[i] > 0:
                self._wait("sp", (f"d{i}", self.dval[i]))

    def emit(self):
        nc = self.nc
        semh = self.semh

        def mk(engname):
            def body(e):
                for o in self.ops[engname]:
                    if o[0] == "w":
                        e.wait_ge(semh[o[1]], o[2])
                    else:
                        o[1](e).then_inc(semh[o[2]], o[3])
            return body

        with nc.Block() as block:
            block.tensor(mk("pe"))
            block.scalar(mk("act"))
            block.vector(mk("dve"))
            block.gpsimd(mk("pool"))
            block.sync(mk("sp"))


class Arena:
    def __init__(self, ap, nwords):
        self.base = ap
        self.n = nwords
        self.off = 0

    def _shape(self, a, shape):
        if len(shape) == 1:
            return a
        if len(shape) == 2:
            return a.rearrange("p (a b) -> p a b", a=shape[0])
        if len(shape) == 3:
            return a.rearrange("p (a b c) -> p a b c", a=shape[0], b=shape[1])
        raise ValueError(shape)

    def f32(self, name, *shape):
        n = int(np.prod(shape))
        assert self.off + n <= self.n, ("arena overflow", name, self.off, n, self.n)
        a = self.base[:, self.off:self.off + n]
        self.off += n
        return Res(name, self._shape(a, shape))

    def bf16(self, name, *shape):
        n = int(np.prod(shape))
        nw = (n + 1) // 2
        assert self.off + nw <= self.n, ("arena overflow", name, self.off, nw, self.n)
        a = self.base[:, self.off:self.off + nw].bitcast(BF16)[:, 0:n]
        self.off += nw
        return Res(name, self._shape(a, shape))


class Ctx:
    pass


def mm(P, out_ap, lhsT, rhs, start, stop, reads, writes, skip=False):
    if skip:
        P.op("pe", lambda e, o=out_ap, l=lhsT, r=rhs, s=start, t=stop: e.matmul(o, l, r, start=s, stop=t, skip_group_check=True),
             reads=reads, writes=writes)
    else:
        P.op("pe", lambda e, o=out_ap, l=lhsT, r=rhs, s=start, t=stop: e.matmul(o, l, r, start=s, stop=t),
             reads=reads, writes=writes)


def emit_transposes(P, C, xt, ntile, xT, evac_flip=0):
    N = ntile * 128
    per_bank = 512 // N if N < 512 else 1
    kc = 0
    i = 0
    while kc < 8:
        bank = C.psT[C.psT_i % len(C.psT)]
        C.psT_i += 1
        g = min(per_bank, 8 - kc)
        for j in range(g):
            for t in range(ntile):
                P.op("pe", lambda e, o=bank.ap[:, j * N + t * 128: j * N + (t + 1) * 128],
                     a=xt.ap[:, t, (kc + j) * 128:(kc + j + 1) * 128], idn=C.ident.ap: e.transpose(o, a, idn),
                     reads=(xt, C.ident), writes=(bank,))
        src = bank.ap[:, 0:g * N].rearrange("p (a b) -> p a b", a=g)
        dst = xT.ap[:, kc:kc + g, 0:N]
        if (i + evac_flip) % 2 == 0:
            P.op("dve", lambda e, o=dst, a=src: e.tensor_copy(o, a), reads=(bank,), writes=(xT,))
        else:
            P.op("act", lambda e, o=dst, a=src: e.activation(out=o, in_=a, func=AF.Copy), reads=(bank,), writes=(xT,))
        kc += g
        i += 1


def emit_layernorm(P, C, xres, xres_ap, Y, gam, bet, xo):
    r, st, mv, xn = C.rbuf, C.stat, C.mv, C.xn
    yap = C.ps_all[:, Y[0].idx * 512: Y[0].idx * 512 + 1024]
    P.op("dve", lambda e, o=r.ap, a=xres_ap, y=yap: e.scalar_tensor_tensor(out=o, in0=a, scalar=ALPHA, in1=y, op0=ALU.mult, op1=ALU.add),
         reads=(xres, Y[0], Y[1]), writes=(r,))
    P.op("dve", lambda e, o=st.ap[:, 0:6], a=r.ap[:, 0:512]: e.bn_stats(o, a), reads=(r,), writes=(st,))
    P.op("dve", lambda e, o=st.ap[:, 6:12], a=r.ap[:, 512:1024]: e.bn_stats(o, a), reads=(r,), writes=(st,))
    P.op("dve", lambda e, o=mv.ap[:, 0:2], a=st.ap[:, 0:12]: e.bn_aggr(o, a), reads=(st,), writes=(mv,))
    P.op("dve", lambda e, o=mv.ap[:, 2:3], a=mv.ap[:, 1:2]: e.tensor_scalar(o, a, LN_EPS, None, op0=ALU.add),
         reads=(mv,), writes=(mv,))
    P.op("act", lambda e, o=mv.ap[:, 2:3], a=mv.ap[:, 2:3]: e.activation(out=o, in_=a, func=AF.Ln), reads=(mv,), writes=(mv,))
    P.op("act", lambda e, o=mv.ap[:, 2:3], a=mv.ap[:, 2:3]: e.activation(out=o, in_=a, func=AF.Exp, scale=-0.5), reads=(mv,), writes=(mv,))
    P.op("dve", lambda e, o=mv.ap[:, 3:4], a=mv.ap[:, 0:1], b=mv.ap[:, 2:3]: e.scalar_tensor_tensor(out=o, in0=a, scalar=-1.0, in1=b, op0=ALU.mult, op1=ALU.mult),
         reads=(mv,), writes=(mv,))
    P.op("act", lambda e, o=xn.ap, a=r.ap, b=mv.ap[:, 3:4], s=mv.ap[:, 2:3]: e.activation(out=o, in_=a, func=AF.Identity, bias=b, scale=s),
         reads=(r, mv), writes=(xn,))
    P.op("pool", lambda e, o=xn.ap, a=xn.ap, g=gam.ap: e.tensor_tensor(o, a, g, op=ALU.mult), reads=(xn, gam), writes=(xn,))
    P.op("pool", lambda e, o=xo.ap, a=xn.ap, b=bet.ap: e.tensor_tensor(o, a, b, op=ALU.add), reads=(xn, bet), writes=(xo,))


def _mkbanks(C, ps):
    C.bank = []
    for i in range(8):
        b = BankRes(f"bank{i}", ps[:, i * 512:(i + 1) * 512])
        b.idx = i
        C.bank.append(b)


def setup_common(P, C, nc, es, arena_words):
    arena_t = es.enter_context(nc.sbuf_tensor("arena", [128, arena_words], F32))
    C.arena = Arena(arena_t[:, :], arena_words)
    ps = es.enter_context(nc.psum_tensor("ps", [128, 4096], F32))
    C.ps_all = ps[:, :]
    _mkbanks(C, ps)


FFN_GROUPS = [(0, 768), (768, 1536), (1536, 2304), (2304, 2816)]


def emit_ffn_sweep(P, C, src_dram, src_res, dst_dram, dst_res, wg_d, wu_d, wd_d, ln_d, ident_d, ntile=2, ntiles=NT, dst_off=0, skip_tiles=0):
    A = C.arena
    A.off = 0
    N = ntile * 128
    Wg = [A.bf16(f"Wg{g}", 8, b - a) for g, (a, b) in enumerate(FFN_GROUPS)]
    Wu = [A.bf16(f"Wu{g}", 8, b - a) for g, (a, b) in enumerate(FFN_GROUPS)]
    Wd = [A.bf16(f"Wd{g}", (b - a) // 128, 1024) for g, (a, b) in enumerate(FFN_GROUPS)]
    xT = A.bf16("xT", 8, N)
    hT = A.bf16("hT", NFC, N)
    C.ident = A.f32("ident", 128)
    gam = A.f32("gam", 1024)
    bet = A.f32("bet", 1024)
    xt = [A.f32(f"xt{i}", ntile, 1024) for i in range(2)]
    C.rbuf = A.f32("r", 1024)
    C.xn = A.f32("xn", 1024)
    xo = [A.f32(f"xo{i}", 1024) for i in range(2)]
    sg = [A.f32(f"sg{i}", N) for i in range(2)]
    C.stat = A.f32("stat", 12)
    C.mv = A.f32("mv", 4)
    C.psT = [C.bank[0], C.bank[7]]
    C.psT_i = 0
    psG = [C.bank[1], C.bank[2]]
    psU = [C.bank[3], C.bank[4]]
    psY = [C.bank[5], C.bank[6]]

    P.dma("sp", C.ident.ap, ident_d, writes=(C.ident,))
    P.dma("sp", gam.ap, ln_d[:, 0:1024], writes=(gam,))
    P.dma("sp", bet.ap, ln_d[:, 1024:2048], writes=(bet,))
    wg3 = wg_d.rearrange("p (k n) -> p k n", k=8)
    wu3 = wu_d.rearrange("p (k n) -> p k n", k=8)
    wd3 = wd_d.rearrange("p (f n) -> p f n", f=NFC)
    blocks = [(t0, min(ntile, ntiles - t0)) for t0 in range(0, ntiles, ntile)]

    def load_x(bi):
        t0, nb = blocks[bi]
        P.dma("sp", xt[bi % 2].ap[:, 0:nb, :], src_dram[t0 * 128:(t0 + nb) * 128, :].rearrange("(t p) d -> p t d", p=128),
              reads=tuple(src_res[t0:t0 + nb]), writes=(xt[bi % 2],))

    load_x(0)
    for g, (a, b) in enumerate(FFN_GROUPS):
        P.dma("pool", Wg[g].ap, wg3[:, :, a:b], writes=(Wg[g],))
        P.dma("pool", Wu[g].ap, wu3[:, :, a:b], writes=(Wu[g],))
    for g, (a, b) in enumerate(FFN_GROUPS):
        P.dma("pool", Wd[g].ap, wd3[:, a // 128:b // 128, :], writes=(Wd[g],))

    for bi, (t0, nb) in enumerate(blocks):
        x = xt[bi % 2]
        Nb = nb * 128
        emit_transposes(P, C, x, nb, xT, evac_flip=bi)
        if bi + 1 < len(blocks):
            load_x(bi + 1)
        for fc in range(NFC):
            g = fc // 6
            c0 = (fc - g * 6) * 128
            G = psG[fc % 2]
            U = psU[fc % 2]
            for kc in range(8):
                mm(P, G.ap[:, 0:Nb], Wg[g].ap[:, kc, c0:c0 + 128], xT.ap[:, kc, 0:Nb], kc == 0, kc == 7, (Wg[g], xT), (G,))
            for kc in range(8):
                mm(P, U.ap[:, 0:Nb], Wu[g].ap[:, kc, c0:c0 + 128], xT.ap[:, kc, 0:Nb], kc == 0, kc == 7, (Wu[g], xT), (U,))
            sgt = sg[fc % 2]
            P.op("act", lambda e, o=sgt.ap[:, 0:Nb], a=G.ap[:, 0:Nb]: e.activation(out=o, in_=a, func=AF.Silu), reads=(G,), writes=(sgt,))
            P.op("dve", lambda e, o=hT.ap[:, fc, 0:Nb], a=sgt.ap[:, 0:Nb], b=U.ap[:, 0:Nb]: e.tensor_tensor(o, a, b, op=ALU.mult),
                 reads=(sgt, U), writes=(hT,))
        for t in range(nb):
            for half in range(2):
                Y = psY[half]
                for fc in range(NFC):
                    g = fc // 6
                    mm(P, Y.ap, hT.ap[:, fc, t * 128:(t + 1) * 128], Wd[g].ap[:, fc - g * 6, half * 512:(half + 1) * 512],
                       fc == 0, fc == NFC - 1, (hT, Wd[g]), (Y,))
            gt = t0 + t
            o = xo[gt % 2]
            emit_layernorm(P, C, x, x.ap[:, t, :], psY, gam, bet, o)
            if gt >= skip_tiles:
                P.dma("sp", dst_dram[(gt + dst_off) * 128:(gt + dst_off + 1) * 128, :], o.ap, reads=(o,), writes=(dst_res[gt + dst_off],))


def emit_tail_sweep(P, C, mix_d, x_d, dst_dram, dst_res, wout_d, ln_d, ident_d, ntiles):
    A = C.arena
    A.off = 0
    Wout = A.bf16("Wout", 8, 1024)
    mT = A.bf16("mT", 8, 512)
    C.ident = A.f32("ident", 128)
    gam = A.f32("gam", 1024)
    bet = A.f32("bet", 1024)
    mt = [A.f32(f"mt{i}", 4, 1024) for i in range(2)]
    xr = [A.f32(f"xr{i}", 4, 1024) for i in range(2)]
    C.rbuf = A.f32("r", 1024)
    C.xn = A.f32("xn", 1024)
    xo = [A.f32(f"xo{i}", 1024) for i in range(2)]
    C.stat = A.f32("stat", 12)
    C.mv = A.f32("mv", 4)
    C.psT = [C.bank[0], C.bank[1]]
    C.psT_i = 0
    psY = [C.bank[4], C.bank[5]]
    P.dma("sp", C.ident.ap, ident_d, writes=(C.ident,))
    P.dma("sp", gam.ap, ln_d[:, 0:1024], writes=(gam,))
    P.dma("sp", bet.ap, ln_d[:, 1024:2048], writes=(bet,))
    P.dma("pool", Wout.ap, wout_d.rearrange("p (k n) -> p k n", k=8), writes=(Wout,))
    blocks = [(t0, min(4, ntiles - t0)) for t0 in range(0, ntiles, 4)]

    def load(bi):
        t0, nb = blocks[bi]
        P.dma("sp", mt[bi % 2].ap[:, 0:nb, :], mix_d[t0 * 128:(t0 + nb) * 128, :].rearrange("(t p) d -> p t d", p=128), writes=(mt[bi % 2],))
        P.dma("sp", xr[bi % 2].ap[:, 0:nb, :], x_d[t0 * 128:(t0 + nb) * 128, :].rearrange("(t p) d -> p t d", p=128), writes=(xr[bi % 2],))

    load(0)
    for bi, (t0, nb) in enumerate(blocks):
        m = mt[bi % 2]
        x = xr[bi % 2]
        emit_transposes(P, C, m, nb, mT, evac_flip=bi)
        if bi + 1 < len(blocks):
            load(bi + 1)
        for t in range(nb):
            for half in range(2):
                Y = psY[half]
                for c in range(8):
                    mm(P, Y.ap, mT.ap[:, c, t * 128:(t + 1) * 128], Wout.ap[:, c, half * 512:(half + 1) * 512], c == 0, c == 7, (mT, Wout), (Y,))
            gt = t0 + t
            o = xo[gt % 2]
            emit_layernorm(P, C, x, x.ap[:, t, :], psY, gam, bet, o)
            P.dma("sp", dst_dram[gt * 128:(gt + 1) * 128, :], o.ap, reads=(o,), writes=(dst_res[gt],))


def emit_even_mixer_sweep(P, C, xh_d, x_d, x_res, dst_dram, dst_res, win_d, wout_d, ln_d, small_d, consts_d):
    A = C.arena
    A.off = 0
    Win = A.bf16("Win", 8, EVEN_IN)
    Wout = A.bf16("Wout", 8, 1024)
    xT = A.bf16("xT", 8, 512)
    yaT = A.bf16("yaT", 4, 512)
    ybT = A.bf16("ybT", 4, 512)
    qT = A.bf16("qT", 4, 512)
    kT = [A.bf16(f"kT{i}", 5 * 128) for i in range(2)]
    vb = [A.bf16(f"vb{i}", 5, 2, 128) for i in range(2)]
    eb = [[A.bf16(f"e{k}{j}", 512) for j in range(2)] for k in range(2)]
    maskC = A.bf16("maskC", 512)
    maskP = A.bf16("maskP", 512)
    maskPF = A.bf16("maskPF", 512)
    onesA = A.bf16("onesA", 128)
    onesB = A.bf16("onesB", 128)
    C.ident = A.f32("ident", 128)
    gam = A.f32("gam", 1024)
    bet = A.f32("bet", 1024)
    small = A.f32("small", 17)
    es4 = A.f32("es4", 4)
    esink = A.f32("esink", 4, 128)
    xt = [A.f32(f"xt{i}", 4, 1024) for i in range(2)]
    zb = [A.f32(f"zb{c}", 514) for c in range(4)]
    csb = A.f32("csb", 512)
    ycv = A.f32("ycv", 512)
    dsb = A.f32("dsb", 512)
    C.rbuf = A.f32("r", 1024)
    C.xn = A.f32("xn", 1024)
    xo = [A.f32(f"xo{i}", 1024) for i in range(2)]
    C.stat = A.f32("stat", 12)
    C.mv = A.f32("mv", 4)
    C.psT = [C.bank[0], C.bank[1]]
    C.psT_i = 0
    psI = [C.bank[0], C.bank[1]]
    psS = [[C.bank[2], C.bank[3]], [C.bank[4], C.bank[5]]]
    psNum, psDen = C.bank[6], C.bank[7]
    psY = [C.bank[4], C.bank[5]]
    pI = [0]

    def nextI():
        b = psI[pI[0] % 2]
        pI[0] += 1
        return b

    P.dma("sp", C.ident.ap, consts_d[:, 0:128], writes=(C.ident,))
    P.dma("sp", small.ap, small_d, writes=(small,))
    P.dma("sp", xt[1].ap[:, 0, :], xh_d, writes=(xt[1],))
    P.dma("pool", Win.ap, win_d.rearrange("p (k n) -> p k n", k=8), writes=(Win,))
    P.dma("pool", maskC.ap, consts_d[:, 128:640], writes=(maskC,))
    P.dma("pool", maskP.ap, consts_d[:, 640:1152], writes=(maskP,))
    P.dma("pool", onesA.ap, consts_d[:, 1152:1280], writes=(onesA,))
    P.dma("pool", onesB.ap, consts_d[:, 1280:1408], writes=(onesB,))
    P.dma("pool", Wout.ap, wout_d.rearrange("p (k n) -> p k n", k=8), writes=(Wout,))
    P.dma("sp", gam.ap, ln_d[:, 0:1024], writes=(gam,))
    P.dma("sp", bet.ap, ln_d[:, 1024:2048], writes=(bet,))
    P.op("act", lambda e, o=es4.ap, a=small.ap[:, 12:16]: e.activation(out=o, in_=a, func=AF.Exp), reads=(small,), writes=(es4,))
    P.op("dve", lambda e, o=esink.ap, a=es4.ap.unsqueeze(2).to_broadcast([128, 4, 128]): e.tensor_copy(o, a), reads=(es4,), writes=(esink,))
    P.op("dve", lambda e, o=maskPF.ap, a=maskP.ap, s=small.ap[:, 16:17]: e.tensor_scalar(o, a, s, None, op0=ALU.mult),
         reads=(maskP, small), writes=(maskPF,))
    for i in range(2):
        P.op("pool", lambda e, o=vb[i].ap: e.memset(o, 0.0), writes=(vb[i],))
    for c in range(4):
        P.op("pool", lambda e, o=zb[c].ap: e.memset(o, 0.0), writes=(zb[c],))

    def load_x(b):
        P.dma("sp", xt[b % 2].ap, x_d[b * 512:(b + 1) * 512, :].rearrange("(t p) d -> p t d", p=128),
              reads=tuple(x_res[b * 4:(b + 1) * 4]), writes=(xt[b % 2],))

    def inproj(col0, N, bank):
        for kc in range(8):
            mm(P, bank.ap[:, 0:N], Win.ap[:, kc, col0:col0 + 128], xT.ap[:, kc, 0:N], kc == 0, kc == 7, (Win, xT), (bank,))

    def block(blk):
        halo = blk < 0
        ntile = 1 if halo else 4
        N = ntile * 128
        x = xt[1] if halo else xt[blk % 2]
        cur = 0 if halo else blk % 2
        slot0 = 0 if halo else 1
        emit_transposes(P, C, x, ntile, xT, evac_flip=blk)
        if halo:
            load_x(0)
        elif blk + 1 < NT // 4:
            load_x(blk + 1)
        for ch in range(4):
            bc = nextI()
            inproj(512 + ch * 128, N, bc)
            P.op("act", lambda e, o=csb.ap[:, 0:N], a=bc.ap[:, 0:N]: e.activation(out=o, in_=a, func=AF.Copy), reads=(bc,), writes=(csb,))
            bh = nextI()
            inproj(1024 + ch * 128, N, bh)
            z = zb[ch]
            P.op("dve", lambda e, o=z.ap[:, 2:2 + N], a=csb.ap[:, 0:N], b=bh.ap[:, 0:N]: e.tensor_tensor(o, a, b, op=ALU.mult),
                 reads=(csb, bh), writes=(z,))
            if not halo:
                bb = nextI()
                inproj(ch * 128, N, bb)
                w = small.ap
                P.op("dve", lambda e, o=ycv.ap, a=z.ap[:, 0:N], s=w[:, ch * 3:ch * 3 + 1]: e.tensor_scalar(o, a, s, None, op0=ALU.mult),
                     reads=(z, small), writes=(ycv,))
                P.op("dve", lambda e, o=ycv.ap, a=z.ap[:, 1:N + 1], s=w[:, ch * 3 + 1:ch * 3 + 2], y=ycv.ap: e.scalar_tensor_tensor(out=o, in0=a, scalar=s, in1=y, op0=ALU.mult, op1=ALU.add),
                     reads=(z, small, ycv), writes=(ycv,))
                P.op("dve", lambda e, o=ycv.ap, a=z.ap[:, 2:N + 2], s=w[:, ch * 3 + 2:ch * 3 + 3], y=ycv.ap: e.scalar_tensor_tensor(out=o, in0=a, scalar=s, in1=y, op0=ALU.mult, op1=ALU.add),
                     reads=(z, small, ycv), writes=(ycv,))
                P.op("dve", lambda e, o=yaT.ap[:, ch, :], a=ycv.ap, b=bb.ap: e.tensor_tensor(o, a, b, op=ALU.mult),
                     reads=(ycv, bb), writes=(yaT,))
            P.op("pool", lambda e, o=z.ap[:, 0:2], a=z.ap[:, N:N + 2]: e.tensor_copy(o, a), reads=(z,), writes=(z,))
            if halo:
                P.op("dve", lambda e, o=z.ap[:, 0:2], a=z.ap[:, 0:2], f=small.ap[:, 16:17]: e.tensor_scalar(o, a, f, None, op0=ALU.mult),
                     reads=(z, small), writes=(z,))
        bk = nextI()
        inproj(2048, N, bk)
        P.op("act", lambda e, o=kT[cur].ap[:, slot0 * 128:slot0 * 128 + N], a=bk.ap[:, 0:N]: e.activation(out=o, in_=a, func=AF.Copy),
             reads=(bk,), writes=(kT[cur],))
        bv = nextI()
        for t in range(ntile):
            for kc in range(8):
                mm(P, bv.ap[:, t * 128:(t + 1) * 128], xT.ap[:, kc, t * 128:(t + 1) * 128], Win.ap[:, kc, 2176:2304], kc == 0, kc == 7, (Win, xT), (bv,))
        bv3 = bv.ap[:, 0:N].rearrange("p (t c) -> p t c", t=ntile)
        P.op("dve", lambda e, o=vb[cur].ap[:, slot0:slot0 + ntile, 0, 0:64], a=bv3[:, :, 0:64]: e.tensor_copy(o, a), reads=(bv,), writes=(vb[cur],))
        P.op("dve", lambda e, o=vb[cur].ap[:, slot0:slot0 + ntile, 1, 64:128], a=bv3[:, :, 64:128]: e.tensor_copy(o, a), reads=(bv,), writes=(vb[cur],))
        if halo:
            return
        for j in range(4):
            bq = nextI()
            inproj(1536 + j * 128, N, bq)
            P.op("act", lambda e, o=qT.ap[:, j, :], a=bq.ap: e.activation(out=o, in_=a, func=AF.Copy, scale=0.125), reads=(bq,), writes=(qT,))
        for t in range(4):
            gt = blk * 4 + t
            for kvh in range(2):
                lo, hi = kvh * 64, (kvh + 1) * 64
                for kb in range(2):
                    slot = t + kb
                    S = psS[kvh][kb]
                    mm(P, S.ap.rearrange("p (j q) -> p j q", j=4), kT[cur].ap[lo:hi, slot * 128:(slot + 1) * 128],
                       qT.ap[lo:hi, :, t * 128:(t + 1) * 128], True, True, (kT[cur], qT), (S,))
                    eb_ = eb[kvh][kb]
                    P.op("act", lambda e, o=eb_.ap, a=S.ap: e.activation(out=o, in_=a, func=AF.Exp), reads=(S,), writes=(eb_,))
                    m = maskC if kb == 1 else (maskPF if gt == 0 else maskP)
                    P.op("pool", lambda e, o=eb_.ap, a=eb_.ap, b=m.ap: e.tensor_tensor(o, a, b, op=ALU.mult), reads=(eb_, m), writes=(eb_,))
            i = 0
            for kvh in range(2):
                for kb in range(2):
                    slot = t + kb
                    mm(P, psNum.ap, vb[cur].ap[:, slot, kvh, :], eb[kvh][kb].ap, i == 0, i == 3, (vb[cur], eb[kvh][kb]), (psNum,))
                    i += 1
            i = 0
            for kvh in range(2):
                for kb in range(2):
                    mm(P, psDen.ap, (onesA if kvh == 0 else onesB).ap, eb[kvh][kb].ap, i == 0, i == 3, (onesA, onesB, eb[kvh][kb]), (psDen,))
                    i += 1
            P.op("dve", lambda e, o=dsb.ap, a=psDen.ap, b=esink.ap.rearrange("p j q -> p (j q)"): e.tensor_tensor(o, a, b, op=ALU.add),
                 reads=(psDen, esink), writes=(dsb,))
            P.op("dve", lambda e, o=dsb.ap, a=dsb.ap: e.reciprocal(o, a), reads=(dsb,), writes=(dsb,))
            P.op("dve", lambda e, o=ybT.ap[:, :, t * 128:(t + 1) * 128], a=psNum.ap.rearrange("p (j q) -> p j q", j=4),
                 b=dsb.ap.rearrange("p (j q) -> p j q", j=4): e.tensor_tensor(o, a, b, op=ALU.mult), reads=(psNum, dsb), writes=(ybT,))
            for half in range(2):
                Y = psY[half]
                for c in range(8):
                    src = yaT if c < 4 else ybT
                    mm(P, Y.ap, src.ap[:, c % 4, t * 128:(t + 1) * 128], Wout.ap[:, c, half * 512:(half + 1) * 512], c == 0, c == 7, (src, Wout), (Y,))
            o = xo[gt % 2]
            emit_layernorm(P, C, x, x.ap[:, t, :], psY, gam, bet, o)
            P.dma("sp", dst_dram[gt * 128:(gt + 1) * 128, :], o.ap, reads=(o,), writes=(dst_res[gt],))
        nxt = 1 - cur
        P.op("pool", lambda e, o=kT[nxt].ap[:, 0:128], a=kT[cur].ap[:, 512:640]: e.tensor_copy(o, a), reads=(kT[cur],), writes=(kT[nxt],))
        P.op("pool", lambda e, o=vb[nxt].ap[:, 0, :, :], a=vb[cur].ap[:, 4, :, :]: e.tensor_copy(o, a), reads=(vb[cur],), writes=(vb[nxt],))

    block(-1)
    for blk in range(NT // 4):
        block(blk)


def lay_kmajor(w):
    K, N = w.shape
    return np.ascontiguousarray(w.reshape(K // 128, 128, N).transpose(1, 0, 2).reshape(128, (K // 128) * N))


def rep128(*vecs):
    return np.ascontiguousarray(np.broadcast_to(np.concatenate(vecs)[None, :], (128, sum(v.shape[0] for v in vecs))))


def make_consts():
    ident = np.eye(128, dtype=np.float32)
    k = np.arange(128)[:, None]
    q = np.arange(128)[None, :]
    mc = (k <= q).astype(np.float32)
    mp = (k > q).astype(np.float32)
    onesA = np.zeros((128, 128), np.float32)
    onesA[:, :64] = 1
    onesB = np.zeros((128, 128), np.float32)
    onesB[:, 64:] = 1
    return np.ascontiguousarray(np.concatenate([ident, np.tile(mc, (1, 4)), np.tile(mp, (1, 4)), onesA, onesB], axis=1))


Q_PERM = np.concatenate([np.concatenate([np.arange(j * 64, (j + 1) * 64), np.arange((j + 4) * 64, (j + 5) * 64)]) for j in range(4)])


def even_weights(inp, i, layer):
    w_in = inp["ev_w_in"][i]
    cols = np.concatenate([np.arange(0, 1536), 1536 + Q_PERM, np.arange(2048, 2304)])
    w_in = w_in[:, cols]
    w_out = inp["ev_w_out"][i]
    rows = np.concatenate([np.arange(0, 512), 512 + Q_PERM])
    w_out = w_out[rows, :]
    cw = inp["ev_conv_w"][i]
    convw = cw.reshape(3, 4, 128).transpose(2, 1, 0).reshape(128, 12)
    sk = inp["ev_sinks"][i]
    sinks = np.concatenate([np.broadcast_to(sk[0:4][None], (64, 4)), np.broadcast_to(sk[4:8][None], (64, 4))], axis=0)
    return dict(
        win=lay_kmajor(w_in), wout=lay_kmajor(w_out),
        ln_mix=rep128(inp["ln_mix_g"][layer], inp["ln_mix_b"][layer]),
        small_base=np.concatenate([convw, sinks], axis=1).astype(np.float32),
    )


def ffn_weights(inp, layer):
    return dict(
        wg=lay_kmajor(inp["ffn_w_gate"][layer]), wu=lay_kmajor(inp["ffn_w_up"][layer]), wd=lay_kmajor(inp["ffn_w_down"][layer]),
        ln_ffn=rep128(inp["ln_ffn_g"][layer], inp["ln_ffn_b"][layer]),
    )


ARENA_WORDS = 49 * 1024


def build_even_layer():
    nc = bass.Bass("TRN2", target_bir_lowering=False)
    dt = lambda name, shape, kind="ExternalInput": nc.dram_tensor(name, shape, F32, kind=kind).ap()
    xh = dt("xh", [128, D])
    x = dt("x", [SEG, D])
    win = dt("win", [128, 8 * EVEN_IN])
    wout = dt("wout", [128, 8 * D])
    ln_mix = dt("ln_mix", [128, 2 * D])
    small = dt("small", [128, 17])
    consts = dt("consts", [128, 1408])
    wg = dt("wg", [128, 8 * DFF])
    wu = dt("wu", [128, 8 * DFF])
    wd = dt("wd", [128, NFC * D])
    ln_ffn = dt("ln_ffn", [128, 2 * D])
    x1 = dt("x1", [SEG, D], kind="Internal")
    y = dt("y", [SEG, D], kind="ExternalOutput")
    with contextlib.ExitStack() as es:
        P = Prog(nc, es)
        C = Ctx()
        setup_common(P, C, nc, es, ARENA_WORDS)
        x_res = [Res(f"x{t}") for t in range(NT)]
        x1_res = [Res(f"x1_{t}") for t in range(NT)]
        y_res = [Res(f"y{t}") for t in range(NT)]
        emit_even_mixer_sweep(P, C, xh, x, x_res, x1, x1_res, win, wout, ln_mix, small, consts)
        P.barrier()
        emit_ffn_sweep(P, C, x1, x1_res, y, y_res, wg, wu, wd, ln_ffn, consts[:, 0:128])
        P.finish()
        P.emit()
    return nc


def run_even_layer(nc, xfull, inp, i, layer):
    ew = even_weights(inp, i, layer)
    fw = ffn_weights(inp, layer)
    consts = make_consts()
    in_maps = []
    for c in range(NCORES):
        b, s = divmod(c, 4)
        own = np.ascontiguousarray(xfull[b, s * SEG:(s + 1) * SEG])
        halo = np.ascontiguousarray(xfull[b, s * SEG - 128:s * SEG]) if s > 0 else np.zeros((128, D), np.float32)
        flag = np.full((128, 1), 1.0 if s > 0 else 0.0, np.float32)
        in_maps.append(dict(xh=halo, x=own, win=ew["win"], wout=ew["wout"], ln_mix=ew["ln_mix"],
                            small=np.ascontiguousarray(np.concatenate([ew["small_base"], flag], axis=1)),
                            consts=consts, wg=fw["wg"], wu=fw["wu"], wd=fw["wd"], ln_ffn=fw["ln_ffn"]))
    res = run_bass_kernel_spmd(nc, in_maps, core_ids=list(range(NCORES)))
    out = np.empty_like(xfull)
    for c in range(NCORES):
        b, s = divmod(c, 4)
        out[b, s * SEG:(s + 1) * SEG] = res.results[c]["y"]
    return out


W1 = 336
W2 = 704
SCALE_MLA = 96 ** -0.5
NTS = SEQ // 128
ODD_NBLK = SEQ // 512


def emit_odd_mixer(P, C, x_d, out_d, wfeat_d, wtok_d, wq_d, wkv_d, wgate_d, gvec_d, rope_d, consts_d):
    A = C.arena
    A.off = 0
    Wf = A.bf16("Wf", 8, W1)
    Wt = A.bf16("Wt", 8, W2)
    Wq = A.bf16("Wq", 2, 192)
    Wkv = A.bf16("Wkv", 224)
    Wga = A.bf16("Wga", 64)
    KT = A.bf16("KT", SEQ)
    Va = A.bf16("Va", NTS, 130)
    xT = A.bf16("xT", 8, 512)
    cqT = A.bf16("cqT", 2, 512)
    cnT = A.bf16("cnT", 512)
    QTs = [A.bf16(f"QT{i}", 512) for i in range(2)]
    glT = A.bf16("glT", 512)
    QD = A.bf16("QD", 2, 128)
    kdT = A.bf16("kdT", 128)
    kdec = A.bf16("kdec", 64)
    vtk = A.bf16("vtk", 128)
    attn = A.bf16("attn", 128)
    Sbf = [A.bf16(f"Sbf{i}", 128) for i in range(2)]
    eb = [A.bf16(f"eb{i}", 512) for i in range(2)]
    C.ident = A.f32("ident", 128)
    LT2 = A.f32("LT2", 128)
    ON2 = A.f32("ON2", 128)
    mBD = A.f32("mBD", 128)
    tri = A.bf16("tri", 128)
    gv = A.f32("gvec", 512)
    xt = [A.f32(f"xt{i}", 4, 1024) for i in range(2)]
    cs = [A.f32(f"cs{i}", 2, 512) for i in range(2)]
    rtmp = A.f32("rtmp", 2, 512)
    lsp = A.f32("lsp", 64)
    btk = A.f32("btk", 64)
    dif = A.f32("dif", 64)
    E1 = A.f32("E1", 128)
    E2 = A.f32("E2", 128)
    S = A.f32("S", 128)
    sr = A.f32("sr", 128)
    cq = A.f32("cq", 384)
    sm = A.f32("sm", 16)
    og = A.f32("og", 128)
    oo = [A.f32(f"oo{i}", 256) for i in range(8)]
    junk = A.f32("junk", 384)

    KTr = [Res(f"KTr{i}") for i in range(SEQ // 512)]
    Var = [Res(f"Var{i}") for i in range(SEQ // 512)]
    bk = C.bank
    ps = C.ps_all

    def sub(bank, a, b, name):
        return SubRes(name, ps[:, bank * 512 + a: bank * 512 + b], bk[bank])
    pF = bk[0]
    pKV3 = sub(2, 0, 320, "pKVR")
    pZ = sub(2, 320, 384, "pZ")
    pBt = sub(2, 384, 448, "pBt")
    pBl = sub(2, 448, 512, "pBl")
    pC = sub(3, 0, 384, "pC")
    pKVs = sub(3, 384, 512, "pKVs")
    pBT = sub(4, 0, 128, "pBT")
    pAt = sub(4, 128, 256, "pAt")
    pO = sub(4, 256, 384, "pO")
    pTr = sub(4, 384, 512, "pTr")
    pS = [bk[5], bk[6]]
    pAcc = [bk[1], bk[7]]

    P.dma("sp", C.ident.ap, consts_d[:, 0:128], writes=(C.ident,))
    P.dma("sp", LT2.ap, consts_d[:, 128:256], writes=(LT2,))
    P.dma("sp", ON2.ap, consts_d[:, 256:384], writes=(ON2,))
    P.dma("sp", mBD.ap, consts_d[:, 384:512], writes=(mBD,))
    P.dma("sp", gv.ap, gvec_d, writes=(gv,))
    P.dma("pool", tri.ap, consts_d[:, 512:640], writes=(tri,))
    P.dma("pool", Wf.ap, wfeat_d.rearrange("p (k n) -> p k n", k=8), writes=(Wf,))
    P.dma("pool", Wt.ap, wtok_d.rearrange("p (k n) -> p k n", k=8), writes=(Wt,))
    P.dma("pool", Wq.ap, wq_d.rearrange("p (k n) -> p k n", k=2), writes=(Wq,))
    P.dma("pool", Wkv.ap, wkv_d, writes=(Wkv,))
    P.dma("pool", Wga.ap, wgate_d, writes=(Wga,))
    P.op("pool", lambda e, o=QD.ap: e.memset(o, 0.0), writes=(QD,))
    P.op("pool", lambda e, o=S.ap: e.memset(o, 0.0), writes=(S,))
    P.op("pool", lambda e, o=Sbf[0].ap: e.memset(o, 0.0), writes=(Sbf[0],))
    P.op("pool", lambda e, o=glT.ap: e.memset(o, 1.0), writes=(glT,))
    P.op("pool", lambda e, o=Va.ap: e.memset(o, 1.0), writes=tuple(Var))

    def load_x(b):
        P.dma("sp", xt[b % 2].ap, x_d[b * 512:(b + 1) * 512, :].rearrange("(t p) d -> p t d", p=128), writes=(xt[b % 2],))
        P.dma("sp", cs[b % 2].ap[64:96, 0, :], rope_d[:, b * 512:(b + 1) * 512], writes=(cs[b % 2],))
        P.dma("sp", cs[b % 2].ap[64:96, 1, :], rope_d[:, SEQ + b * 512:SEQ + (b + 1) * 512], writes=(cs[b % 2],))

    def featproj(c0, M, out_ap, writes, start=True, stop=True):
        for kc in range(8):
            mm(P, out_ap, Wf.ap[:, kc, c0:c0 + M], xT.ap[:, kc, :], start and kc == 0, stop and kc == 7, (Wf, xT), writes)

    def rms_rstd(src_ap, src_res, n, col):
        P.op("act", lambda e, o=junk.ap[:, 0:n], a=src_ap, acc=sm.ap[:, col:col + 1]: e.activation(out=o, in_=a, func=AF.Square, accum_out=acc),
             reads=(src_res,), writes=(junk, sm))
        P.op("act", lambda e, o=sm.ap[:, col:col + 1], a=sm.ap[:, col:col + 1]: e.activation(out=o, in_=a, func=AF.Ln, bias=RMS_EPS, scale=1.0 / n),
             reads=(sm,), writes=(sm,))
        P.op("act", lambda e, o=sm.ap[:, col:col + 1], a=sm.ap[:, col:col + 1]: e.activation(out=o, in_=a, func=AF.Exp, scale=-0.5),
             reads=(sm,), writes=(sm,))

    def attention(blk, QT):
        nk = 4 * blk + 4
        for j in range(nk):
            Sb = pS[j % 2]
            e_ = eb[j % 2]
            jj = j - 4 * blk
            mm(P, Sb.ap, KT.ap[0:96, j * 128:(j + 1) * 128], QT.ap[0:96, :], True, True, (KTr[j // 4], QT), (Sb,))
            q0 = max(jj, 0)
            P.op("act", lambda e, o=e_.ap[:, q0 * 128:512], a=Sb.ap[:, q0 * 128:512]: e.activation(out=o, in_=a, func=AF.Exp, scale=SCALE_MLA),
                 reads=(Sb,), writes=(e_,))
            if jj >= 0:
                P.op("pool", lambda e, o=e_.ap[:, jj * 128:(jj + 1) * 128], a=e_.ap[:, jj * 128:(jj + 1) * 128], b=tri.ap: e.tensor_tensor(o, a, b, op=ALU.mult),
                     reads=(e_, tri), writes=(e_,))
            for t in range(q0, 4):
                acc = pAcc[t // 2]
                c0 = (t % 2) * 130
                mm(P, acc.ap[:, c0:c0 + 130], e_.ap[:, t * 128:(t + 1) * 128], Va.ap[:, j, 0:130], j == 0 and t % 2 == 0, j == 4 * blk + t, (e_, Var[j // 4]), (acc,), skip=True)
            yield
        for t in range(4):
            gt = blk * 4 + t
            acc = pAcc[t // 2]
            c0 = (t % 2) * 130
            ob = oo[gt % 8]
            P.op("dve", lambda e, o=sm.ap[:, 4 + t:5 + t], a=acc.ap[:, c0 + 128:c0 + 129]: e.reciprocal(o, a), reads=(acc,), writes=(sm,))
            P.op("dve", lambda e, o=ob.ap[:, 128:256], a=acc.ap[:, c0:c0 + 128], s=sm.ap[:, 4 + t:5 + t]: e.tensor_scalar(o, a, s, None, op0=ALU.mult),
                 reads=(acc, sm), writes=(ob,))
            P.dma("sp", out_d[gt * 128:(gt + 1) * 128, :], ob.ap, reads=(ob,))
        yield

    pend = []
    CHAIN_OPS = 300.0

    load_x(0)
    for blk in range(ODD_NBLK):
        x = xt[blk % 2]
        csb = cs[blk % 2]
        QT = QTs[blk % 2]
        if pend:
            gen = pend.pop(0)
            rate = (4 * (blk - 1) + 5) / CHAIN_OPS
            credit = [0.0]

            def tick(gen=gen, rate=rate, credit=credit):
                credit[0] += rate
                while credit[0] >= 1.0:
                    credit[0] -= 1.0
                    next(gen, None)
            P.after_op = tick
        C.psT = [pF]
        C.psT_i = 0
        emit_transposes(P, C, x, 4, xT, evac_flip=blk)
        if blk + 1 < ODD_NBLK:
            load_x(blk + 1)
        featproj(128, 32, pF.ap[0:32, :], (pF,))
        P.op("act", lambda e, o=glT.ap[0:16, :], a=pF.ap[0:16, :]: e.activation(out=o, in_=a, func=AF.Copy), reads=(pF,), writes=(glT,))
        for t in range(4):
            gt = blk * 4 + t
            tok = slice(t * 128, (t + 1) * 128)
            for kc in range(8):
                mm(P, pKV3.ap, xT.ap[:, kc, tok], Wt.ap[:, kc, 0:320], kc == 0, kc == 7, (Wt, xT), (pKV3,))
            for kc in range(8):
                mm(P, pC.ap, xT.ap[:, kc, tok], Wt.ap[:, kc, 320:704], kc == 0, kc == 7, (Wt, xT), (pC,))
            mm(P, pZ.ap, glT.ap[0:32, tok], Wga.ap[0:32, :], True, True, (glT, Wga), (pZ,))
            P.op("act", lambda e, o=lsp.ap, a=pZ.ap: e.activation(out=o, in_=a, func=AF.Exp, scale=-1.0), reads=(pZ,), writes=(lsp,))
            P.op("act", lambda e, o=lsp.ap, a=lsp.ap: e.activation(out=o, in_=a, func=AF.Ln, bias=1.0), reads=(lsp,), writes=(lsp,))
            mm(P, pBt.ap, LT2.ap, lsp.ap, True, True, (LT2, lsp), (pBt,))
            mm(P, pBl.ap, ON2.ap, lsp.ap, True, True, (ON2, lsp), (pBl,))
            mm(P, pBT.ap[0:64, :], lsp.ap, LT2.ap, True, True, (LT2, lsp), (pBT,))
            P.op("act", lambda e, o=E1.ap[0:64, :], a=pBT.ap[0:64, :]: e.activation(out=o, in_=a, func=AF.Exp), reads=(pBT,), writes=(E1,))
            P.op("act", lambda e, o=E2.ap[0:64, :], a=pBT.ap[0:64, :]: e.activation(out=o, in_=a, func=AF.Exp, scale=-1.0), reads=(pBT,), writes=(E2,))
            P.op("act", lambda e, o=btk.ap, a=pBt.ap: e.activation(out=o, in_=a, func=AF.Copy), reads=(pBt,), writes=(btk,))
            P.op("dve", lambda e, o=dif.ap, a=pBl.ap, b=btk.ap: e.tensor_tensor(o, a, b, op=ALU.subtract), reads=(pBl, btk), writes=(dif,))
            P.op("act", lambda e, o=dif.ap, a=dif.ap: e.activation(out=o, in_=a, func=AF.Exp), reads=(dif,), writes=(dif,))
            P.op("dve", lambda e, o=kdec.ap, a=pKV3.ap[:, 0:64], b=dif.ap: e.tensor_tensor(o, a, b, op=ALU.mult), reads=(pKV3, dif), writes=(kdec,))
            P.op("act", lambda e, o=vtk.ap, a=pKV3.ap[:, 64:192]: e.activation(out=o, in_=a, func=AF.Copy), reads=(pKV3,), writes=(vtk,))
            P.op("act", lambda e, o=sr.ap, a=pKV3.ap[:, 192:320]: e.activation(out=o, in_=a, func=AF.Exp, scale=-1.0), reads=(pKV3,), writes=(sr,))
            P.op("dve", lambda e, o=sr.ap, a=sr.ap: e.tensor_scalar(o, a, 1.0, None, op0=ALU.add), reads=(sr,), writes=(sr,))
            P.op("dve", lambda e, o=sr.ap, a=sr.ap: e.reciprocal(o, a), reads=(sr,), writes=(sr,))
            P.op("dve", lambda e, o=sr.ap, a=pKV3.ap[:, 192:320], b=sr.ap: e.tensor_tensor(o, a, b, op=ALU.mult), reads=(pKV3, sr), writes=(sr,))
            for kc in range(8):
                mm(P, pF.ap[0:64, 0:128], Wf.ap[:, kc, 0:64], xT.ap[:, kc, tok], kc == 0, kc == 7, (Wf, xT), (pF,))
            for kc in range(8):
                mm(P, pF.ap[0:64, 128:256], Wf.ap[:, kc, 64:128], xT.ap[:, kc, tok], kc == 0, kc == 7, (Wf, xT), (pF,))
            P.op("dve", lambda e, o=QD.ap[0:64, 0, 0:64], a=pF.ap[0:64, 0:64], b=E1.ap[0:64, 0:64]: e.scalar_tensor_tensor(out=o, in0=a, scalar=0.125, in1=b, op0=ALU.mult, op1=ALU.mult),
                 reads=(pF, E1), writes=(QD,))
            P.op("dve", lambda e, o=QD.ap[0:64, 1, 64:128], a=pF.ap[0:64, 64:128], b=E1.ap[0:64, 64:128]: e.scalar_tensor_tensor(out=o, in0=a, scalar=0.125, in1=b, op0=ALU.mult, op1=ALU.mult),
                 reads=(pF, E1), writes=(QD,))
            P.op("dve", lambda e, o=kdT.ap[0:64, :], a=pF.ap[0:64, 128:256], b=E2.ap[0:64, :]: e.tensor_tensor(o, a, b, op=ALU.mult), reads=(pF, E2), writes=(kdT,))
            mm(P, pAt.ap, kdT.ap[0:64, :], QD.ap[0:64, 0, :], True, False, (kdT, QD), (pAt,))
            mm(P, pAt.ap, kdT.ap[0:64, :], QD.ap[0:64, 1, :], False, True, (kdT, QD), (pAt,))
            P.op("dve", lambda e, o=attn.ap, a=pAt.ap, b=mBD.ap: e.tensor_tensor(o, a, b, op=ALU.mult), reads=(pAt, mBD), writes=(attn,))
            s0 = Sbf[0]
            s1 = Sbf[1]
            mm(P, pO.ap, attn.ap, vtk.ap, True, False, (attn, vtk), (pO,))
            mm(P, pO.ap, QD.ap[0:64, 0, :], s0.ap[0:64, :], False, False, (QD, s0), (pO,))
            mm(P, pKVs.ap[0:64, :], kdec.ap[0:64, :], vtk.ap[0:64, :], True, True, (kdec, vtk), (pKVs,))
            P.op("dve", lambda e, o=S.ap[0:64, :], a=S.ap[0:64, :], d=E1.ap[0:64, 63:64], kv=pKVs.ap[0:64, :]: e.scalar_tensor_tensor(out=o, in0=a, scalar=d, in1=kv, op0=ALU.mult, op1=ALU.add),
                 reads=(S, E1, pKVs), writes=(S,))
            P.op("act", lambda e, o=s1.ap[0:64, :], a=S.ap[0:64, :]: e.activation(out=o, in_=a, func=AF.Copy), reads=(S,), writes=(s1,))
            mm(P, pO.ap, QD.ap[0:64, 1, :], s1.ap[0:64, :], False, True, (QD, s1), (pO,))
            mm(P, pKVs.ap[0:64, :], kdec.ap[64:128, :], vtk.ap[64:128, :], True, True, (kdec, vtk), (pKVs,))
            P.op("dve", lambda e, o=S.ap[0:64, :], a=S.ap[0:64, :], d=E1.ap[0:64, 127:128], kv=pKVs.ap[0:64, :]: e.scalar_tensor_tensor(out=o, in0=a, scalar=d, in1=kv, op0=ALU.mult, op1=ALU.add),
                 reads=(S, E1, pKVs), writes=(S,))
            P.op("act", lambda e, o=s0.ap[0:64, :], a=S.ap[0:64, :]: e.activation(out=o, in_=a, func=AF.Copy), reads=(S,), writes=(s0,))
            ob = oo[gt % 8]
            rms_rstd(pO.ap, pO, 128, 0)
            P.op("dve", lambda e, o=og.ap, a=pO.ap, s=sm.ap[:, 0:1], g=gv.ap[:, 384:512]: e.scalar_tensor_tensor(out=o, in0=a, scalar=s, in1=g, op0=ALU.mult, op1=ALU.mult),
                 reads=(pO, sm, gv), writes=(og,))
            P.op("pool", lambda e, o=ob.ap[:, 0:128], a=og.ap, b=sr.ap: e.tensor_tensor(o, a, b, op=ALU.mult), reads=(og, sr), writes=(ob,))
            rms_rstd(pC.ap[:, 0:256], pC, 256, 1)
            rms_rstd(pC.ap[:, 256:384], pC, 128, 2)
            P.op("dve", lambda e, o=cq.ap[:, 0:256], a=pC.ap[:, 0:256], s=sm.ap[:, 1:2], g=gv.ap[:, 0:256]: e.scalar_tensor_tensor(out=o, in0=a, scalar=s, in1=g, op0=ALU.mult, op1=ALU.mult),
                 reads=(pC, sm, gv), writes=(cq,))
            P.op("dve", lambda e, o=cq.ap[:, 256:384], a=pC.ap[:, 256:384], s=sm.ap[:, 2:3], g=gv.ap[:, 256:384]: e.scalar_tensor_tensor(out=o, in0=a, scalar=s, in1=g, op0=ALU.mult, op1=ALU.mult),
                 reads=(pC, sm, gv), writes=(cq,))
            for j in range(3):
                P.op("pe", lambda e, o=pTr.ap, a=cq.ap[:, j * 128:(j + 1) * 128], idn=C.ident.ap: e.transpose(o, a, idn), reads=(cq, C.ident), writes=(pTr,))
                dst = cqT.ap[:, j, tok] if j < 2 else cnT.ap[:, tok]
                dres = cqT if j < 2 else cnT
                P.op("act", lambda e, o=dst, a=pTr.ap: e.activation(out=o, in_=a, func=AF.Copy), reads=(pTr,), writes=(dres,))
            mm(P, pTr.ap, cnT.ap[:, tok], Wkv.ap[:, 96:224], True, True, (cnT, Wkv), (pTr,))
            P.op("act", lambda e, o=Va.ap[:, gt, 0:128], a=pTr.ap: e.activation(out=o, in_=a, func=AF.Copy), reads=(pTr,), writes=(Var[blk],))
        mm(P, pF.ap[0:96, :], Wkv.ap[:, 0:96], cnT.ap, True, False, (Wkv, cnT), (pF,))
        featproj(144, 96, pF.ap[0:96, :], (pF,), start=False, stop=True)
        P.op("act", lambda e, o=KT.ap[0:64, blk * 512:(blk + 1) * 512], a=pF.ap[0:64, :]: e.activation(out=o, in_=a, func=AF.Copy), reads=(pF,), writes=(KTr[blk],))
        P.op("dve", lambda e, o=rtmp.ap[64:96, 0, :], a=pF.ap[64:96, :], b=csb.ap[64:96, 0, :]: e.tensor_tensor(o, a, b, op=ALU.mult), reads=(pF, csb), writes=(rtmp,))
        featproj(240, 96, pF.ap[0:96, :], (pF,))
        P.op("dve", lambda e, o=rtmp.ap[64:96, 1, :], a=pF.ap[64:96, :], b=csb.ap[64:96, 1, :]: e.tensor_tensor(o, a, b, op=ALU.mult), reads=(pF, csb), writes=(rtmp,))
        P.op("dve", lambda e, o=KT.ap[64:96, blk * 512:(blk + 1) * 512], a=rtmp.ap[64:96, 0, :], b=rtmp.ap[64:96, 1, :]: e.tensor_tensor(o, a, b, op=ALU.add), reads=(rtmp,), writes=(KTr[blk],))
        for c in range(2):
            mm(P, pF.ap[0:96, :], Wq.ap[:, c, 0:96], cqT.ap[:, c, :], c == 0, c == 1, (Wq, cqT), (pF,))
        P.op("act", lambda e, o=QT.ap[0:64, :], a=pF.ap[0:64, :]: e.activation(out=o, in_=a, func=AF.Copy), reads=(pF,), writes=(QT,))
        P.op("dve", lambda e, o=rtmp.ap[64:96, 0, :], a=pF.ap[64:96, :], b=csb.ap[64:96, 0, :]: e.tensor_tensor(o, a, b, op=ALU.mult), reads=(pF, csb), writes=(rtmp,))
        for c in range(2):
            mm(P, pF.ap[0:96, :], Wq.ap[:, c, 96:192], cqT.ap[:, c, :], c == 0, c == 1, (Wq, cqT), (pF,))
        P.op("dve", lambda e, o=rtmp.ap[64:96, 1, :], a=pF.ap[64:96, :], b=csb.ap[64:96, 1, :]: e.tensor_tensor(o, a, b, op=ALU.mult), reads=(pF, csb), writes=(rtmp,))
        P.op("dve", lambda e, o=QT.ap[64:96, :], a=rtmp.ap[64:96, 0, :], b=rtmp.ap[64:96, 1, :]: e.tensor_tensor(o, a, b, op=ALU.add), reads=(rtmp,), writes=(QT,))
        if P.after_op is not None:
            P.after_op = None
            for _ in gen:
                pass
        pend.append(attention(blk, QT))
    for g in pend:
        for _ in g:
            pass


def odd_mixer_weights(inp, i, h):
    w_in = inp["od_w_in"][i]
    z64 = np.zeros((D, 64), np.float32)
    kr = w_in[:, 1936:1968]
    kr_sw = np.concatenate([kr[:, 16:32], kr[:, 0:16]], axis=1)
    wfeat = np.concatenate([w_in[:, h * 64:(h + 1) * 64], w_in[:, 256 + h * 64:256 + (h + 1) * 64], w_in[:, 1024:1040],
                            z64, kr, z64, kr_sw], axis=1)
    wtok = np.concatenate([w_in[:, 256 + h * 64:256 + (h + 1) * 64], w_in[:, 512 + h * 128:512 + (h + 1) * 128],
                           w_in[:, 1040 + h * 128:1040 + (h + 1) * 128], w_in[:, 1552:1808], w_in[:, 1808:1936]], axis=1)
    wuq = inp["od_mla_w_uq"][i][:, h * 96:(h + 1) * 96]
    rp = wuq[:, 64:96]
    rp_sw = np.concatenate([rp[:, 16:32], rp[:, 0:16]], axis=1)
    z256 = np.zeros((256, 64), np.float32)
    wq = np.concatenate([wuq, z256, rp_sw], axis=1)
    wukv = inp["od_mla_w_ukv"][i][:, h * 192:(h + 1) * 192]
    wkv = np.concatenate([wukv[:, 0:64], np.zeros((128, 32), np.float32), wukv[:, 64:192]], axis=1)
    wg = np.zeros((128, 64), np.float32)
    wg[0:16] = inp["od_gla_w_gate"][i][:, h * 64:(h + 1) * 64]
    wg[16] = inp["od_gla_b_gate"][i][h * 64:(h + 1) * 64]
    return dict(wfeat=lay_kmajor(wfeat), wtok=lay_kmajor(wtok), wq=lay_kmajor(wq), wkv=np.ascontiguousarray(wkv), wgate=wg,
                gvec=rep128(inp["od_mla_q_norm_g"][i], inp["od_mla_kv_norm_g"][i], inp["od_gla_norm_g"][i]))


def odd_consts():
    ident = np.eye(128, dtype=np.float32)
    s = np.arange(128)[:, None]
    t = np.arange(128)[None, :]
    same = (s // 64) == (t // 64)
    lt2 = (same & (s <= t)).astype(np.float32) * np.float32(-1.0 / 16.0)
    on2 = same.astype(np.float32) * np.float32(-1.0 / 16.0)
    mbd = (same & (s <= t)).astype(np.float32)
    tri = (s <= t).astype(np.float32)
    pos = np.arange(SEQ, dtype=np.float32)
    inv_freq = (np.float32(10000.0) ** (-np.arange(0, 32, 2, dtype=np.float32) / np.float32(32))).astype(np.float32)
    ang = (pos[:, None] * inv_freq[None, :]).astype(np.float32)
    cos, sin = np.cos(ang).astype(np.float32), np.sin(ang).astype(np.float32)
    rope = np.concatenate([np.concatenate([cos, cos], axis=1).T, np.concatenate([-sin, sin], axis=1).T], axis=1)
    return np.ascontiguousarray(np.concatenate([ident, lt2, on2, mbd, tri], axis=1)), np.ascontiguousarray(rope.astype(np.float32))


def build_odd_mixer():
    nc = bass.Bass("TRN2", target_bir_lowering=False)
    dt = lambda name, shape, kind="ExternalInput": nc.dram_tensor(name, shape, F32, kind=kind).ap()
    x = dt("x", [ODD_NBLK * 512, D])
    wfeat = dt("wfeat", [128, 8 * W1])
    wtok = dt("wtok", [128, 8 * W2])
    wq = dt("wq", [128, 2 * 192])
    wkv = dt("wkv", [128, 224])
    wgate = dt("wgate", [128, 64])
    gvec = dt("gvec", [128, 512])
    rope = dt("rope", [32, 2 * SEQ])
    consts = dt("consts", [128, 640])
    out = dt("out", [ODD_NBLK * 512, 256], kind="ExternalOutput")
    with contextlib.ExitStack() as es:
        P = Prog(nc, es)
        C = Ctx()
        setup_common(P, C, nc, es, ARENA_WORDS)
        emit_odd_mixer(P, C, x, out, wfeat, wtok, wq, wkv, wgate, gvec, rope, consts)
        P.finish()
        P.emit()
    return nc


def run_odd_mixer(nc, xfull, inp, i):
    consts, rope = odd_consts()
    in_maps = []
    for c in range(NCORES):
        b, h = divmod(c, 4)
        w = odd_mixer_weights(inp, i, h)
        in_maps.append(dict(x=np.ascontiguousarray(xfull[b][:ODD_NBLK * 512]), rope=rope, consts=consts, **w))
    res = run_bass_kernel_spmd(nc, in_maps, core_ids=list(range(NCORES)))
    mix = np.zeros((2, SEQ, D), np.float32)
    n = ODD_NBLK * 512
    for c in range(NCORES):
        b, h = divmod(c, 4)
        o = res.results[c]["out"]
        mix[b, :n, h * 128:(h + 1) * 128] = o[:, 0:128]
        mix[b, :n, 512 + h * 128:512 + (h + 1) * 128] = o[:, 128:256]
    return mix


def build_tail(with_even):
    nt = NT + 1 if with_even else NT
    nc = bass.Bass("TRN2", target_bir_lowering=False)
    dt = lambda name, shape, kind="ExternalInput": nc.dram_tensor(name, shape, F32, kind=kind).ap()
    mix = dt("mix", [nt * 128, D])
    xr = dt("xr", [nt * 128, D])
    o_wout = dt("o_wout", [128, 8 * D])
    o_ln_mix = dt("o_ln_mix", [128, 2 * D])
    o_wg = dt("o_wg", [128, 8 * DFF])
    o_wu = dt("o_wu", [128, 8 * DFF])
    o_wd = dt("o_wd", [128, NFC * D])
    o_ln_ffn = dt("o_ln_ffn", [128, 2 * D])
    consts = dt("consts", [128, 1408])
    xa = dt("xa", [nt * 128, D], kind="Internal")
    y = dt("y", [SEG, D], kind="ExternalOutput")
    if with_even:
        xb = dt("xb", [nt * 128, D], kind="Internal")
        xc = dt("xc", [SEG, D], kind="Internal")
        win = dt("win", [128, 8 * EVEN_IN])
        wout = dt("wout", [128, 8 * D])
        ln_mix = dt("ln_mix", [128, 2 * D])
        small = dt("small", [128, 17])
        wg = dt("wg", [128, 8 * DFF])
        wu = dt("wu", [128, 8 * DFF])
        wd = dt("wd", [128, NFC * D])
        ln_ffn = dt("ln_ffn", [128, 2 * D])
    with contextlib.ExitStack() as es:
        P = Prog(nc, es)
        C = Ctx()
        setup_common(P, C, nc, es, ARENA_WORDS)
        xa_res = [Res(f"xa{t}") for t in range(nt)]
        emit_tail_sweep(P, C, mix, xr, xa, xa_res, o_wout, o_ln_mix, consts[:, 0:128], nt)
        P.barrier()
        if not with_even:
            y_res = [Res(f"y{t}") for t in range(NT)]
            emit_ffn_sweep(P, C, xa, xa_res, y, y_res, o_wg, o_wu, o_wd, o_ln_ffn, consts[:, 0:128], ntiles=nt)
        else:
            xb_res = [Res(f"xb{t}") for t in range(nt)]
            emit_ffn_sweep(P, C, xa, xa_res, xb, xb_res, o_wg, o_wu, o_wd, o_ln_ffn, consts[:, 0:128], ntiles=nt)
            P.barrier()
            xc_res = [Res(f"xc{t}") for t in range(NT)]
            y_res = [Res(f"y{t}") for t in range(NT)]
            emit_even_mixer_sweep(P, C, xb[0:128, :], xb[128:, :], xb_res[1:], xc, xc_res, win, wout, ln_mix, small, consts)
            P.barrier()
            emit_ffn_sweep(P, C, xc, xc_res, y, y_res, wg, wu, wd, ln_ffn, consts[:, 0:128])
        P.finish()
        P.emit()
    return nc


def run_tail(nc, mixfull, xfull, inp, i_odd, layer_odd, with_even, i_even=None, layer_even=None):
    consts = make_consts()
    ow = dict(o_wout=lay_kmajor(inp["od_w_out"][i_odd]),
              o_ln_mix=rep128(inp["ln_mix_g"][layer_odd], inp["ln_mix_b"][layer_odd]))
    fo = ffn_weights(inp, layer_odd)
    ow.update(o_wg=fo["wg"], o_wu=fo["wu"], o_wd=fo["wd"], o_ln_ffn=fo["ln_ffn"])
    if with_even:
        ew = even_weights(inp, i_even, layer_even)
        fw = ffn_weights(inp, layer_even)
    in_maps = []
    for c in range(NCORES):
        b, sg_ = divmod(c, 4)
        lo = sg_ * SEG
        if with_even:
            if sg_ > 0:
                m = np.ascontiguousarray(mixfull[b, lo - 128:lo + SEG])
                xx = np.ascontiguousarray(xfull[b, lo - 128:lo + SEG])
            else:
                m = np.concatenate([np.zeros((128, D), np.float32), mixfull[b, lo:lo + SEG]], axis=0)
                xx = np.concatenate([np.zeros((128, D), np.float32), xfull[b, lo:lo + SEG]], axis=0)
            flag = np.full((128, 1), 1.0 if sg_ > 0 else 0.0, np.float32)
            d = dict(mix=m, xr=xx, consts=consts, win=ew["win"], wout=ew["wout"], ln_mix=ew["ln_mix"],
                     small=np.ascontiguousarray(np.concatenate([ew["small_base"], flag], axis=1)),
                     wg=fw["wg"], wu=fw["wu"], wd=fw["wd"], ln_ffn=fw["ln_ffn"], **ow)
        else:
            d = dict(mix=np.ascontiguousarray(mixfull[b, lo:lo + SEG]), xr=np.ascontiguousarray(xfull[b, lo:lo + SEG]), consts=consts, **ow)
        in_maps.append(d)
    res = run_bass_kernel_spmd(nc, in_maps, core_ids=list(range(NCORES)))
    out = np.empty_like(xfull)
    for c in range(NCORES):
        b, sg_ = divmod(c, 4)
        out[b, sg_ * SEG:(sg_ + 1) * SEG] = res.results[c]["y"]
    return out


def kernel(**inputs):
    inp = {k: np.asarray(v) for k, v in inputs.items()}
    x0 = np.ascontiguousarray(inp["x"], dtype=np.float32)
    nc_even = build_even_layer()
    x1 = run_even_layer(nc_even, x0, inp, 0, 0)
    nc_odd = build_odd_mixer()
    mix1 = run_odd_mixer(nc_odd, x1, inp, 0)
    nc_te = build_tail(True)
    x3 = run_tail(nc_te, mix1, x1, inp, 0, 1, True, 1, 2)
    mix3 = run_odd_mixer(nc_odd, x3, inp, 1)
    nc_t = build_tail(False)
    x4 = run_tail(nc_t, mix3, x3, inp, 1, 3, False)
    return x4.astype(np.float32)
```
